# Optimizing a Trainium2 kernel written in Bass

```python
import jax, jax.numpy as jnp
from jax import lax
import numpy as np

D_MODEL = 1024
BATCH = 1
SEQ = 16384
DEPTH = 4
DEC_BATCH = 4
DEC_SEQ = 8192
PAST_LEN = 128

N_MEM = 256
D_CONV_A = 512
CONV_A_W = 3
N_HEADS_B = 8
N_KV_B = 2
HEAD_DIM_B = 64
WINDOW = 128
BLOCK = 128
N_BUCKETS = 32
MAX_DIST = 128
D_Q_B = N_HEADS_B * HEAD_DIM_B
D_KV_B = N_KV_B * HEAD_DIM_B
D_IN_AB = 3 * D_CONV_A + D_Q_B + 2 * D_KV_B
D_MIX_AB = D_CONV_A + D_Q_B
CONV_C_W = 31
N_HEADS_X = 4
HEAD_DIM_X = 128
D_X = N_HEADS_X * HEAD_DIM_X
D_FF = 2816
CONV_F_W = 3
N_EVEN = (DEPTH + 1) // 2
N_ODD = DEPTH // 2
N_NORMS = 6
EPS = 1e-6
NEG = -1e30

kernel_name = "hybrid_conv_swa_conformer_encoder"


def rmsnorm(x, g):
    x32 = x.astype(jnp.float32)
    y = x32 * lax.rsqrt(jnp.mean(x32 * x32, axis=-1, keepdims=True) + EPS)
    return (y * g.astype(jnp.float32)).astype(x.dtype)


def layernorm(x, g, b):
    x32 = x.astype(jnp.float32)
    mu = jnp.mean(x32, axis=-1, keepdims=True)
    xc = x32 - mu
    y = xc * lax.rsqrt(jnp.mean(xc * xc, axis=-1, keepdims=True) + EPS)
    return (y * g.astype(jnp.float32) + b.astype(jnp.float32)).astype(x.dtype)


def dwconv(x, w):
    width, ch = w.shape
    return lax.conv_general_dilated(
        x, w.astype(x.dtype)[:, None, :], window_strides=(1,),
        padding=[(width // 2, width // 2)],
        dimension_numbers=("NWC", "WIO", "NWC"), feature_group_count=ch)


def t5_bucket(rel):
    half = N_BUCKETS // 2
    max_exact = half // 2
    ret = (rel > 0).astype(jnp.int32) * half
    n = jnp.abs(rel)
    nf = jnp.maximum(n, 1).astype(jnp.float32)
    large = max_exact + (jnp.log(nf / max_exact) / np.float32(np.log(MAX_DIST / max_exact))
                         * (half - max_exact)).astype(jnp.int32)
    large = jnp.minimum(large, half - 1)
    return ret + jnp.where(n < max_exact, n, large)


def block_bias(rel_table):
    qi = jnp.arange(BLOCK)[:, None]
    kj = jnp.arange(3 * BLOCK)[None, :]
    rel = kj - BLOCK - qi
    bias = rel_table.astype(jnp.float32)[t5_bucket(rel)]
    return jnp.transpose(bias, (2, 0, 1)), jnp.abs(rel) <= WINDOW


def windowed_gqa(q, k, v, sink, bias, band):
    b, s = q.shape[:2]
    nb = s // BLOCK
    g = N_HEADS_B // N_KV_B
    qb = q.reshape(b, nb, BLOCK, N_KV_B, g, HEAD_DIM_B)

    def band_blocks(t):
        tp = jnp.pad(t, ((0, 0), (BLOCK, BLOCK), (0, 0), (0, 0)))
        return jnp.concatenate(
            [tp[:, o * BLOCK:o * BLOCK + s].reshape(b, nb, BLOCK, N_KV_B, HEAD_DIM_B) for o in range(3)],
            axis=2)

    kb, vb = band_blocks(k), band_blocks(v)
    logits = jnp.einsum("bnqkgd,bnjkd->bnkgqj", qb, kb,
                        preferred_element_type=jnp.float32) * np.float32(HEAD_DIM_B ** -0.5)
    logits = logits + bias.reshape(N_KV_B, g, BLOCK, 3 * BLOCK)
    key_pos = jnp.arange(nb)[:, None] * BLOCK + jnp.arange(3 * BLOCK)[None, :] - BLOCK
    valid = band[None] & ((key_pos >= 0) & (key_pos < s))[:, None, :]
    logits = jnp.where(valid[None, :, None, None], logits, NEG)
    sink_l = sink.astype(jnp.float32).reshape(N_KV_B, g)[None, None, :, :, None, None]
    m = jnp.maximum(jnp.max(logits, axis=-1, keepdims=True), sink_l)
    p = jnp.exp(logits - m)
    denom = jnp.sum(p, axis=-1, keepdims=True) + jnp.exp(sink_l - m)
    o = jnp.einsum("bnkgqj,bnjkd->bnqkgd", (p / denom).astype(v.dtype), vb)
    return o.reshape(b, s, D_Q_B)


def mixer_ab(x, w_in, conv_w, sink, w_out, bias, band):
    b, s, _ = x.shape
    z = x @ w_in
    c0 = D_CONV_A
    gb, gc, xa, q, k, v = jnp.split(
        z, [c0, 2 * c0, 3 * c0, 3 * c0 + D_Q_B, 3 * c0 + D_Q_B + D_KV_B], axis=-1)
    ya = gb * dwconv(gc * xa, conv_w)
    yb = windowed_gqa(q.reshape(b, s, N_HEADS_B, HEAD_DIM_B),
                      k.reshape(b, s, N_KV_B, HEAD_DIM_B),
                      v.reshape(b, s, N_KV_B, HEAD_DIM_B), sink, bias, band)
    return jnp.concatenate([ya, yb], axis=-1) @ w_out


def conformer_conv(x, w_pw1, conv_w, ln_g, ln_b, w_pw2):
    a, gt = jnp.split(x @ w_pw1, 2, axis=-1)
    u = a * jax.nn.sigmoid(gt)
    u = dwconv(u, conv_w)
    u = layernorm(u, ln_g, ln_b)
    return jax.nn.silu(u) @ w_pw2


def mem_xattn(x, mem, w_q, w_kv, w_o):
    b, s, _ = x.shape
    nm = mem.shape[1]
    q = (x @ w_q).reshape(b, s, N_HEADS_X, HEAD_DIM_X)
    k, v = jnp.split(mem @ w_kv, 2, axis=-1)
    k = k.reshape(b, nm, N_HEADS_X, HEAD_DIM_X)
    v = v.reshape(b, nm, N_HEADS_X, HEAD_DIM_X)
    logits = jnp.einsum("bshd,bmhd->bhsm", q, k,
                        preferred_element_type=jnp.float32) * np.float32(HEAD_DIM_X ** -0.5)
    p = jax.nn.softmax(logits, axis=-1).astype(v.dtype)
    o = jnp.einsum("bhsm,bmhd->bshd", p, v).reshape(b, s, D_X)
    return o @ w_o


def conv_ffn(x, w_up, conv_w, w_down):
    h = dwconv(x @ w_up, conv_w)
    g, u = jnp.split(h, 2, axis=-1)
    return (jax.nn.silu(g) * u) @ w_down


def trunk(h, mem, norm_g, rel_bias, w_in_ab, conv_a, sink_b, w_out_ab,
          w_pw1_c, conv_c, ln_g_c, ln_b_c, w_pw2_c, w_xq, w_xkv, w_xo, w_up, conv_f, w_down):
    bias, band = block_bias(rel_bias)
    for i in range(DEPTH):
        g = norm_g[i]
        j = i // 2
        hn = rmsnorm(h, g[0])
        if i % 2 == 0:
            t = mixer_ab(hn, w_in_ab[j], conv_a[j], sink_b[j], w_out_ab[j], bias, band)
        else:
            t = conformer_conv(hn, w_pw1_c[j], conv_c[j], ln_g_c[j], ln_b_c[j], w_pw2_c[j])
        h = h + rmsnorm(t, g[1])
        h = h + rmsnorm(mem_xattn(rmsnorm(h, g[2]), mem, w_xq[i], w_xkv[i], w_xo[i]), g[3])
        h = h + rmsnorm(conv_ffn(rmsnorm(h, g[4]), w_up[i], conv_f[i], w_down[i]), g[5])
    return h


def setup_inputs(seed: int = 0) -> dict:
    key = jax.random.key(seed)
    ks = jax.random.split(key, 24)
    f32 = jnp.float32

    def nrm(k, shape, scale):
        return jax.random.normal(k, shape, f32) * np.float32(scale)

    D = D_MODEL
    return {
        "x_prompt": nrm(ks[0], (BATCH, SEQ, D), 1.0),
        "x_sample": nrm(ks[1], (DEC_BATCH, DEC_SEQ, D), 1.0),
        "mem_prompt": nrm(ks[2], (BATCH, N_MEM, D), 1.0),
        "mem_sample": nrm(ks[3], (DEC_BATCH, N_MEM, D), 1.0),
        "norm_g": 1.0 + nrm(ks[4], (DEPTH, N_NORMS, D), 0.1),
        "rel_bias": nrm(ks[5], (N_BUCKETS, N_HEADS_B), 0.5),
        "w_in_ab": nrm(ks[6], (N_EVEN, D, D_IN_AB), D ** -0.5),
        "conv_a": nrm(ks[7], (N_EVEN, CONV_A_W, D_CONV_A), CONV_A_W ** -0.5),
        "sink_b": nrm(ks[8], (N_EVEN, N_HEADS_B), 0.5),
        "w_out_ab": nrm(ks[9], (N_EVEN, D_MIX_AB, D), D_MIX_AB ** -0.5),
        "w_pw1_c": nrm(ks[10], (N_ODD, D, 2 * D), D ** -0.5),
        "conv_c": nrm(ks[11], (N_ODD, CONV_C_W, D), CONV_C_W ** -0.5),
        "ln_g_c": 1.0 + nrm(ks[12], (N_ODD, D), 0.1),
        "ln_b_c": nrm(ks[13], (N_ODD, D), 0.01),
        "w_pw2_c": nrm(ks[14], (N_ODD, D, D), D ** -0.5),
        "w_xq": nrm(ks[15], (DEPTH, D, D_X), D ** -0.5),
        "w_xkv": nrm(ks[16], (DEPTH, D, 2 * D_X), D ** -0.5),
        "w_xo": nrm(ks[17], (DEPTH, D_X, D), D_X ** -0.5),
        "w_up": nrm(ks[18], (DEPTH, D, 2 * D_FF), D ** -0.5),
        "conv_f": nrm(ks[19], (DEPTH, CONV_F_W, 2 * D_FF), CONV_F_W ** -0.5),
        "w_down": nrm(ks[20], (DEPTH, D_FF, D), D_FF ** -0.5),
    }


def reference(x_prompt, x_sample, mem_prompt, mem_sample, norm_g, rel_bias, w_in_ab, conv_a,
              sink_b, w_out_ab, w_pw1_c, conv_c, ln_g_c, ln_b_c, w_pw2_c, w_xq, w_xkv, w_xo,
              w_up, conv_f, w_down):
    y_prompt = trunk(x_prompt, mem_prompt, norm_g, rel_bias, w_in_ab, conv_a, sink_b, w_out_ab,
                     w_pw1_c, conv_c, ln_g_c, ln_b_c, w_pw2_c, w_xq, w_xkv, w_xo, w_up, conv_f, w_down)
    y_sample = trunk(x_sample, mem_sample, norm_g, rel_bias, w_in_ab, conv_a, sink_b, w_out_ab,
                     w_pw1_c, conv_c, ln_g_c, ln_b_c, w_pw2_c, w_xq, w_xkv, w_xo, w_up, conv_f, w_down)
    return (y_prompt, y_sample)
```

```python
import numpy as np
from contextlib import ExitStack
import concourse.bass as bass
import concourse.mybir as mybir
from concourse.bass_utils import run_bass_kernel_spmd

F32 = mybir.dt.float32
BF16 = mybir.dt.bfloat16
AF = mybir.ActivationFunctionType
ALU = mybir.AluOpType

D = 1024
T = 2560
NTC = 3
HALO = 290
PAD = 16
XW = PAD + 1280 + 128 + PAD
DEPTH = 4
EPS = 1e-6
DFF = 2816
NCORES = 8


class Sched:
    LIMIT = 30000 - 30000 % 16

    def __init__(self, nc, stack, inc_record=None, inc_targets=None):
        self.nc = nc
        self.stack = stack
        self.inc_record = inc_record
        self.inc_targets = inc_targets
        self.cand = {}
        self.engs = {"pe": nc.tensor, "act": nc.scalar, "dve": nc.vector, "pool": nc.gpsimd, "sp": nc.sync}
        self.sem, self.cnt, self.epoch, self.step = {}, {}, {}, {}
        self.waited = {e: {} for e in self.engs}
        self.last_write = {}
        self.readers = {}
        self.nsem = 0
        for e in self.engs:
            self._newprod(e, 1)

    def _newsem(self, name):
        self.nsem += 1
        return self.stack.enter_context(self.nc.semaphore(f"s{self.nsem}_{name}"))

    def _newprod(self, p, step):
        self.sem[p] = [self._newsem(p)]
        self.cnt[p] = 0
        self.epoch[p] = 0
        self.step[p] = step

    def dma_group(self, name):
        self._newprod(name, 16)
        return name

    def _deps(self, e, reads, writes):
        deps = {}

        def add(t):
            p, ep, c = t
            cur = deps.get(p)
            if cur is None or (ep, c) > cur:
                deps[p] = (ep, c)

        for r in reads:
            lw = self.last_write.get(r)
            if lw is not None and not (e == "pe" and lw[0] == "pe"):
                add(lw)
        for w in writes:
            lw = self.last_write.get(w)
            if lw is not None and lw[0] != e:
                add(lw)
            for rd in self.readers.get(w, ()):
                if rd[0] != e:
                    add(rd)
        return deps

    def _emit_waits(self, e, deps):
        eng = self.engs[e]
        for p, (ep, c) in deps.items():
            cur = self.waited[e].get(p)
            if cur is not None and cur >= (ep, c):
                continue
            if self.inc_record is not None and p in self.engs:
                self.inc_record.setdefault(p, set()).add(ep * self.LIMIT + c)
            if self.inc_targets is not None and p in self.engs:
                assert (ep, c) <= (self.epoch[p], self.cnt[p]) or p == e, ("wait on un-emitted increment", e, p, ep, c)
            eng.wait_ge(self.sem[p][ep], c)
            self.waited[e][p] = (ep, c)

    def _commit(self, p, ins, inc):
        if inc:
            ins.then_inc(self.sem[p][self.epoch[p]], self.step[p])
            self.cnt[p] += self.step[p]
            if self.cnt[p] >= self.LIMIT:
                self.sem[p].append(self._newsem(p))
                self.epoch[p] += 1
                self.cnt[p] = 0

    def _record(self, t, reads, writes):
        for r in reads:
            lst = self.readers.setdefault(r, [])
            lst[:] = [x for x in lst if x[0] != t[0]]
            lst.append(t)
        for w in writes:
            self.last_write[w] = t
            self.readers[w] = []

    def op(self, e, fn, reads=(), writes=(), inc=True):
        deps = self._deps(e, reads, writes)
        self._emit_waits(e, deps)
        t = (e, self.epoch[e], self.cnt[e] + 1)
        if inc and self.inc_targets is not None:
            k = self.cand.get(e, 0) + 1
            self.cand[e] = k
            if k not in self.inc_targets.get(e, ()):
                inc = False
        ins = fn(self.engs[e])
        self._commit(e, ins, inc)
        self._record(t, reads, writes)
        return t

    def dma(self, group, out, in_, reads=(), writes=(), e="sp"):
        deps = self._deps(group, reads, writes)
        self._emit_waits(e, deps)
        t = (group, self.epoch[group], self.cnt[group] + 16)
        ins = self.engs[e].dma_start(out=out, in_=in_)
        self._commit(group, ins, True)
        self._record(t, reads, writes)
        return t

    def fence(self, extra=(), engines=("act", "dve", "pool")):
        prods = ["pe", "act", "dve", "pool"] + list(extra)
        for e in engines:
            deps = {}
            for p in prods:
                if p == e:
                    continue
                ep, c = self.epoch[p], self.cnt[p]
                if c == 0 and ep > 0:
                    ep, c = ep - 1, self.LIMIT
                if c > 0:
                    deps[p] = (ep, c)
            self._emit_waits(e, deps)

    def wait_all(self, e, resources):
        deps = {}
        for r in resources:
            lw = self.last_write.get(r)
            if lw is not None:
                p, ep, c = lw
                if p not in deps or (ep, c) > deps[p]:
                    deps[p] = (ep, c)
        self._emit_waits(e, deps)


def split_cols(n, mx=512):
    out, c = [], 0
    while c < n:
        w = min(mx, n - c)
        out.append((c, w))
        c += w
    return out


def blocks_of(t0, n):
    return range(t0 // 128, (t0 + n - 1) // 128 + 1)


def build_program(n_sub=12 * 1, n_tiles=NTC, depth=DEPTH, wseq=None, record=None, inc_record=None, inc_targets=None):
    nc = bass.Bass("TRN2", target_bir_lowering=False)
    dr = lambda n, s, d, k: nc.dram_tensor(n, s, d, kind=k)
    xt = dr("xt", [NTC, D, T], F32, "ExternalInput").ap()
    memt = dr("memt", [NTC, D, 256], F32, "ExternalInput").ap()
    yt = dr("yt", [NTC, D, T], F32, "ExternalOutput").ap()
    w32 = {}
    wshapes = {"w_in_ab": [2, D, 2304], "w_out_ab": [2, D, D], "w_pw1_c": [2, D, 2 * D], "w_pw2_c": [2, D, D],
               "w_xq": [4, D, 512], "w_xkv": [4, D, D], "w_xo": [4, 512, D], "w_up": [4, D, 2 * DFF],
               "w_down": [4, DFF, D]}
    for k, s in wshapes.items():
        w32[k] = dr(k, s, F32, "ExternalInput").ap()
    cshapes = {"gains": [128, 4 * 6 * 8], "cva": [128, 2 * 3 * 4], "cvc": [128, 2 * 31 * 8], "lng": [128, 16],
               "lnb": [128, 16], "cvf": [128, 4 * 3 * 44], "sinkrow": [1, 2 * 2 * 512], "biasT": [128, 8 * 384],
               "bmask": [128, 384], "ident": [128, 128]}
    cin = {k: dr(k, s, F32, "ExternalInput").ap() for k, s in cshapes.items()}

    pieces = {}

    def add_piece(pid, parts, kc, cols, srcs):
        t = dr("sc_" + pid, [parts, kc * cols], BF16, "Internal").ap()
        pieces[pid] = dict(dram=t, parts=parts, kc=kc, cols=cols, srcs=srcs)

    def kview(w, l, r0, nrows, c0, c1, p=128):
        return w[l, r0:r0 + nrows, c0:c1].rearrange("(k p) n -> p k n", p=p)

    for j in range(2):
        wi = w32["w_in_ab"]
        for nm, c0 in (("gb", 0), ("gc", 512), ("xa", 1024), ("q", 1536)):
            add_piece(f"ab{j}_{nm}", 128, 8, 512, [((0, 512), kview(wi, j, 0, D, c0, c0 + 512))])
        add_piece(f"ab{j}_kv", 128, 8, 384, [
            ((0, 64), kview(wi, j, 0, D, 2048, 2112)), ((64, 128), kview(wi, j, 0, D, 2048, 2112)),
            ((128, 192), kview(wi, j, 0, D, 2112, 2176)), ((192, 256), kview(wi, j, 0, D, 2112, 2176)),
            ((256, 384), kview(wi, j, 0, D, 2176, 2304))])
        wo = w32["w_out_ab"]
        add_piece(f"ab{j}_woA", 128, 4, 1024, [((0, 1024), kview(wo, j, 0, 512, 0, 1024))])
        for hh in range(2):
            add_piece(f"ab{j}_woB{hh}", 64, 8, 512,
                      [((0, 512), wo[j, 512:1024, hh * 512:(hh + 1) * 512].rearrange("(h d) n -> d h n", d=64))])
        w1 = w32["w_pw1_c"]
        for pj in range(4):
            add_piece(f"cf{j}_pw1_{pj}", 128, 8, 512, [
                ((0, 256), kview(w1, j, 0, D, pj * 256, pj * 256 + 256)),
                ((256, 512), kview(w1, j, 0, D, 1024 + pj * 256, 1024 + pj * 256 + 256))])
        for pj in range(2):
            add_piece(f"cf{j}_pw2_{pj}", 128, 8, 512, [((0, 512), kview(w32["w_pw2_c"], j, 0, D, pj * 512, pj * 512 + 512))])
        for c in range(8):
            add_piece(f"cf{j}_dg{c}", 128, 31, 128, [])
    for l in range(4):
        add_piece(f"x{l}_q", 128, 8, 512, [((0, 512), kview(w32["w_xq"], l, 0, D, 0, 512))])
        add_piece(f"x{l}_k", 128, 8, 512, [((0, 512), kview(w32["w_xkv"], l, 0, D, 0, 512))])
        add_piece(f"x{l}_v", 128, 8, 512, [((0, 512), kview(w32["w_xkv"], l, 0, D, 512, 1024))])
        add_piece(f"x{l}_o", 128, 4, 1024, [((0, 1024), kview(w32["w_xo"], l, 0, 512, 0, 1024))])
        wu = w32["w_up"]
        for pj in range(11):
            add_piece(f"f{l}_up{pj}", 128, 8, 512, [
                ((0, 256), kview(wu, l, 0, D, pj * 256, pj * 256 + 256)),
                ((256, 512), kview(wu, l, 0, D, DFF + pj * 256, DFF + pj * 256 + 256))])
        for oc in range(8):
            add_piece(f"f{l}_dn{oc}", 128, 22, 128, [((0, 128), kview(w32["w_down"], l, 0, DFF, oc * 128, oc * 128 + 128))])

    with ExitStack() as st:
        S = Sched(nc, st, inc_record=inc_record, inc_targets=inc_targets)
        sb = lambda n, s, d: st.enter_context(nc.sbuf_tensor(n, s, d))
        PS = st.enter_context(nc.psum_tensor("PS", [128, 8, 512], F32))
        H = sb("H", [128, 8, T], F32)
        XN = sb("XN", [128, 8, XW], BF16)
        XH = sb("XH", [128, 8, 128], BF16)
        SLOTS = [sb(f"slot{i}", [128, 4096], BF16) for i in range(3)]
        G = sb("G", [128, 192], F32)
        CVA = sb("CVA", [128, 24], F32)
        CVC = sb("CVC", [128, 496], F32)
        LNG = sb("LNG", [128, 16], F32)
        LNB = sb("LNB", [128, 16], F32)
        CVF = sb("CVF", [128, 528], F32)
        ESROW = sb("ESROW", [1, 2048], BF16)
        EB = sb("EB", [128, 8, 384], BF16)
        ONES = sb("ONES", [128, 128], BF16)
        ONESD = sb("ONESD", [128, 128], BF16)
        IDENT = sb("IDENT", [128, 128], BF16)
        EPSC = sb("EPSC", [128, 1], F32)
        SQ = sb("SQ", [128, 8, 256], BF16)
        RS = sb("RS", [128, 2, 256], F32)
        TMP = sb("TMP", [128, 2, 256], F32)
        reg_base = 229344 - nc.sbuf_bytes_remaining
        reg_base = (reg_base + 31) // 32 * 32
        REG = sb("REG", [128, 14208 + 16], F32)
        REGF = nc.alloc_sbuf_tensor_at("REGF", [128, 14208], F32, offset=reg_base)
        REGB = nc.alloc_sbuf_tensor_at("REGB", [128, 28416], BF16, offset=reg_base)

        def rv(off, n, dt):
            if dt == F32:
                return REGF[:, off // 4:off // 4 + n]
            return REGB[:, off // 2:off // 2 + n]

        HIDv = rv(0, 22 * 640, BF16).rearrange("p (c n) -> p c n", c=22)
        TSv = rv(28160, 8 * 640, F32).rearrange("p (c n) -> p c n", c=8)
        YGb = [rv(48640, 648, BF16), rv(52528, 648, BF16)]
        YUb = [rv(49936, 648, BF16), rv(53824, 648, BF16)]
        CGb = [rv(51232, 648, BF16), rv(55120, 648, BF16)]
        Ubuf = rv(0, 8 * 1344, BF16).rearrange("p (c n) -> p c n", c=8)
        CO = rv(21504, 8 * 1280, BF16).rearrange("p (c n) -> p c n", c=8)
        SG = rv(41984, 512, F32)
        LT = rv(44032, 1024, F32).rearrange("p (a n) -> p a n", a=4)
        DT = rv(48128, 512, F32).rearrange("p (a n) -> p a n", a=2)
        PC = rv(0, 4 * 1440, BF16).rearrange("p (c n) -> p c n", c=4)
        KT = rv(11520, 2 * 1408, BF16).rearrange("p (g n) -> p g n", g=2)
        QB = rv(17152, 4 * 1280, BF16).rearrange("p (c n) -> p c n", c=4)
        GB = rv(27392, 4 * 1280, BF16).rearrange("p (c n) -> p c n", c=4)
        PT5 = [rv(37632 + 6144 * b, 3 * 8 * 128, BF16).rearrange("p (j c t q) -> p j c t q", j=3, c=4, t=2) for b in range(2)]
        PTh = [rv(37632 + 6144 * b, 3 * 8 * 128, BF16).rearrange("p (j h q) -> p j h q", j=3, h=8) for b in range(2)]
        YB = rv(49920, 8 * 256, BF16).rearrange("p (h n) -> p h n", h=8)
        VV = rv(54016, 11 * 128, BF16).rearrange("p (b n) -> p b n", b=11)
        RDEN = rv(0, 512, F32)
        QXb = [rv(4096 * b, 4 * 512, BF16).rearrange("p (h n) -> p h n", h=4) for b in range(2)]
        PXb = [rv(8192 + 2048 * b, 2 * 512, BF16).rearrange("p (m n) -> p m n", m=2) for b in range(2)]
        OXb = [rv(12288 + 4096 * b, 4 * 512, BF16).rearrange("p (h n) -> p h n", h=4) for b in range(2)]
        RDb = [rv(20480 + 2048 * b, 512, F32) for b in range(2)]
        MEMB = rv(24576, 8 * 256, BF16).rearrange("p (k m) -> p k m", k=8)
        KX = rv(28672, 4 * 256, BF16).rearrange("p (h m) -> p h m", h=4)
        VX = rv(30720, 2 * 512, BF16).rearrange("p (m n) -> p m n", m=2)
        STG = rv(32768, 2048, F32)
        DGSf = rv(0, 31 * 128, BF16)
        BT = rv(8192, 3072, F32)

        EBB = EB[:].rearrange("p h x -> p (h x)").rearrange("p (t o c q) -> p t o c q", t=2, o=3, c=4)
        gq = [S.dma_group(f"gw{i}") for i in range(3)]
        g_misc = S.dma_group("gmisc")
        g_cv = S.dma_group("gcv")
        g_cvs = [S.dma_group("gcv0"), S.dma_group("gcv1")]
        cv_state = {"k": 0, "last": [None, None], "dg": None}
        g_io = S.dma_group("gio")
        g_ios = [S.dma_group(f"gio{q}") for q in range(4)]

        psc = {"f": 0, "h": 0}

        pinned = set()

        def ps_full():
            while True:
                b = psc["f"] % 8
                psc["f"] += 1
                if b not in pinned:
                    break
            return PS[:, b, :], [("ps", b)], b

        def ps_half(allow=None):
            while True:
                i = psc["h"] % 16
                psc["h"] += 1
                b, hf = i // 2, i % 2
                if b not in pinned or (allow is not None and b in allow):
                    break
            return PS[:, b, hf * 256:(hf + 1) * 256], [("ps", b)]

        def t_tiles():
            own = set()
            tiles = []
            while len(tiles) < 8:
                i = psc["h"] % 16
                b, hf = i // 2, i % 2
                if hf == 1 and b not in own:
                    psc["h"] += 1
                    continue
                if b in pinned and b not in own:
                    psc["h"] += 1
                    continue
                psc["h"] += 1
                own.add(b)
                pinned.add(b)
                tiles.append((PS[:, b, hf * 256:(hf + 1) * 256], [("ps", b)]))
            return tiles, own

        def mm(out_ap, out_res, pairs, reads, pair_reads=None):
            n = len(pairs)
            for i, (l, r) in enumerate(pairs):
                rd = list(reads) if i == 0 else []
                if pair_reads is not None:
                    rd += list(pair_reads[i])
                S.op("pe", lambda e, l=l, r=r, i=i: e.matmul(out_ap, lhsT=l, rhs=r, start=(i == 0), stop=(i == n - 1)),
                     reads=rd, writes=out_res, inc=(i == n - 1))

        def xnr(c0, n):
            return [("XN", b) for b in range(c0 // 16, (c0 + n - 1) // 16 + 1)]

        def ld(dst, src, res):
            S.dma(g_misc, dst, src, writes=[res])

        import os
        DBG = os.environ.get("KDBG", "")
        ld(G[:], cin["gains"], "G"); ld(CVA[:], cin["cva"], "CVA"); ld(CVC[:], cin["cvc"], "CVC")
        ld(LNG[:], cin["lng"], "LNG"); ld(LNB[:], cin["lnb"], "LNB"); ld(CVF[:], cin["cvf"], "CVF")
        S.wait_all("sp", ["G", "CVA", "CVC", "LNG", "LNB", "CVF"])
        S.op("dve", lambda e: e.memset(EPSC[:], EPS), writes=["EPSC"])
        S.op("dve", lambda e: e.memset(ONES[:], 1.0), writes=["ONES"])
        S.op("dve", lambda e: e.memset(ONESD[:], 1.0 / D), writes=["ONESD"])
        S.op("dve", lambda e: e.memset(XN[:], 0.0), writes=xnr(0, XW))
        ld(STG[:, 0:128], cin["ident"], "STG")
        S.op("dve", lambda e: e.tensor_copy(IDENT[:], STG[:, 0:128]), reads=["STG"], writes=["IDENT"])
        if "a" not in DBG:
            ld(STG[0:1, 0:2048], cin["sinkrow"], "STG")
            S.op("act", lambda e: e.activation(ESROW[:], STG[0:1, 0:2048], AF.Exp), reads=["STG"], writes=["ESROW"])
        ld(BT, cin["biasT"], "R3")
        ld(STG[:, 0:384], cin["bmask"], "STG")
        NEGT = STG[:, 512:896]
        S.op("dve", lambda e: e.tensor_scalar(NEGT, STG[:, 0:384], 240000.0, -240000.0, op0=ALU.mult, op1=ALU.add), reads=["STG"], writes=["NEGT"])
        for h in range(8):
            S.op("dve", lambda e, h=h: e.scalar_tensor_tensor(BT[:, h * 384:(h + 1) * 384], BT[:, h * 384:(h + 1) * 384], 8.0, STG[:, 0:384], op0=ALU.mult, op1=ALU.mult),
                 reads=["R3", "STG"], writes=["R3"])
            S.op("dve", lambda e, h=h: e.tensor_tensor(EBB[:, h % 2, :, h // 2, :], BT[:, h * 384:(h + 1) * 384].rearrange("p (o q) -> p o q", o=3),
                                                       NEGT.rearrange("p (o q) -> p o q", o=3), op=ALU.add),
                 reads=["R3", "NEGT"], writes=["EB"])

        def convert(pid):
            p = pieces[pid]
            dv = p["dram"].rearrange("p (k n) -> p k n", k=p["kc"])
            for (c0, c1), src in p["srcs"]:
                gi_ = cv_state["k"] % 2
                cv_state["k"] += 1
                lt = cv_state["last"][gi_]
                if lt is not None:
                    S._emit_waits("pool", {lt[0]: (lt[1], lt[2])})
                cv_state["last"][gi_] = S.dma(g_cvs[gi_], dv[:, :, c0:c1], src, reads=[("sc", pid)] if len(p["srcs"]) > 1 else (), writes=[("sc", pid)], e="pool")


        DGS = DGSf.rearrange("p (k n) -> p k n", k=31)

        def build_diag(j, c):
            for k in range(31):
                S.op("dve", lambda e, k=k: e.tensor_scalar(DGS[:, k, :], IDENT[:], CVC[:, (j * 31 + k) * 8 + c:(j * 31 + k) * 8 + c + 1],
                                                           None, op0=ALU.mult), reads=["IDENT", "CVC"], writes=["DGS"])
            if cv_state["dg"] is not None:
                lt = cv_state["dg"]
                S._emit_waits("sp", {lt[0]: (lt[1], lt[2])})
            cv_state["dg"] = S.dma(g_cv, pieces[f"cf{j}_dg{c}"]["dram"], DGSf, reads=["DGS"], writes=[("sc", f"cf{j}_dg{c}")])

        def layer_pids(l):
            j = l // 2
            out = []
            if l % 2 == 0:
                out += [f"ab{j}_{n}" for n in ("kv", "gc", "xa", "gb", "q", "woA", "woB0", "woB1")]
            else:
                out += [f"cf{j}_pw1_{i}" for i in range(4)] + [f"cf{j}_pw2_{i}" for i in range(2)]
            out += [f"x{l}_k", f"x{l}_v", f"x{l}_q", f"x{l}_o"]
            out += [f"f{l}_up{i}" for i in range(11)] + [f"f{l}_dn{i}" for i in range(8)]
            return out

        nlay = min(depth, (n_sub + 2) // 3)
        for l in range(nlay):
            for pid in layer_pids(l):
                convert(pid)
            if l % 2 == 1:
                for c in range(8):
                    build_diag(l // 2, c)

        slot_pid = [None, None, None]
        slot_use = [0, 0, 0]
        clock = [0]

        wptr = [0]

        def _load(pid, i):
            p = pieces[pid]
            n = p["kc"] * p["cols"]
            S.dma(gq[i], SLOTS[i][0:p["parts"], 0:n], p["dram"], reads=[("sc", pid)], writes=[("slot", i)])
            slot_pid[i] = pid

        def wslot(pid, pin=()):
            clock[0] += 1
            if record is not None:
                record.append((pid, tuple(pin)))
            if pid in slot_pid:
                i = slot_pid.index(pid)
            else:
                cands = [i for i in range(3) if slot_pid[i] not in pin]
                i = min(cands, key=lambda i: slot_use[i])
                _load(pid, i)
            slot_use[i] = clock[0]
            cur = wptr[0]
            wptr[0] += 1
            if wseq is not None:
                assert wseq[cur][0] == pid, (cur, wseq[cur], pid)
                keep = {pid} | set(pin)
                m = cur + 1
                while m < len(wseq) and m < cur + 40:
                    npid, npin = wseq[m]
                    if npid in slot_pid:
                        keep.add(npid)
                        keep |= set(npin)
                        m += 1
                        continue
                    cands = [k for k in range(3) if slot_pid[k] not in keep]
                    if not cands:
                        break
                    k = min(cands, key=lambda k: slot_use[k])
                    _load(npid, k)
                    slot_use[k] = clock[0]
                    keep.add(npid)
                    keep |= set(npin)
                    m += 1
            p = pieces[pid]
            v = SLOTS[i][0:p["parts"], 0:p["kc"] * p["cols"]].rearrange("p (k n) -> p k n", k=p["kc"])
            return v, ("slot", i)

        def gidx(l, ni, c):
            return (l * 6 + ni) * 8 + c

        def Hres(t0, n):
            return [("H", b) for b in blocks_of(t0, n)]

        def pre_tasks(l, ni, t0, n, col0, sq_eng="act"):
            tasks = []
            for (u0, un) in split_cols(n, 256):
                a = t0 + u0
                hres = Hres(a, un)
                st = {}

                def p1(a=a, un=un, hres=hres):
                    if sq_eng == "pool":
                        S.op("pool", lambda e: e.tensor_tensor(SQ[:, :, 0:un], H[:, :, a:a + un], H[:, :, a:a + un], op=ALU.mult), reads=hres, writes=["SQ"])
                    else:
                        S.op("act", lambda e: e.activation(SQ[:, :, 0:un], H[:, :, a:a + un], AF.Square), reads=hres, writes=["SQ"])

                def p2(un=un, st=st):
                    pm, pres = ps_half()
                    mm(pm[:, 0:un], pres, [(ONESD[:], SQ[:, c, 0:un]) for c in range(8)], ["SQ", "ONESD"])
                    S.op("act", lambda e: e.activation(RS[:, 0, 0:un], pm[:, 0:un], AF.Ln, bias=EPSC[:, 0:1]), reads=pres + ["EPSC"], writes=["RS0"])
                    S.op("act", lambda e: e.activation(RS[:, 1, 0:un], RS[:, 0, 0:un], AF.Exp, scale=-0.5), reads=["RS0"], writes=["RS1"])

                def p3(a=a, un=un, u0=u0, hres=hres, cs=range(8)):
                    for c in cs:
                        S.op("dve", lambda e, c=c: e.scalar_tensor_tensor(
                            XN[:, c, col0 + u0:col0 + u0 + un], H[:, c, a:a + un], G[:, gidx(l, ni, c):gidx(l, ni, c) + 1],
                            RS[:, 1, 0:un], op0=ALU.mult, op1=ALU.mult), reads=hres + ["G", "RS1"], writes=xnr(col0 + u0, un))
                tasks += [p1, p2, (lambda p3=p3: p3(cs=range(0, 4))), (lambda p3=p3: p3(cs=range(4, 8)))]
            return tasks

        def prenorm(l, ni, t0, n, col0):
            for t in pre_tasks(l, ni, t0, n, col0):
                t()

        def post_tasks(l, ni, t0, un, srcs, src_res, sq_eng="act"):
            def p1():
                for c in range(8):
                    if sq_eng == "pool":
                        S.op("pool", lambda e, c=c: e.tensor_tensor(SQ[:, c, 0:un], srcs[c], srcs[c], op=ALU.mult), reads=src_res[c], writes=["SQ"])
                    else:
                        S.op("act", lambda e, c=c: e.activation(SQ[:, c, 0:un], srcs[c], AF.Square), reads=src_res[c], writes=["SQ"])

            def p2():
                pm, pres = ps_half()
                mm(pm[:, 0:un], pres, [(ONESD[:], SQ[:, c, 0:un]) for c in range(8)], ["SQ", "ONESD"])
                S.op("act", lambda e: e.activation(RS[:, 0, 0:un], pm[:, 0:un], AF.Ln, bias=EPSC[:, 0:1]), reads=pres + ["EPSC"], writes=["RS0"])
                S.op("act", lambda e: e.activation(RS[:, 1, 0:un], RS[:, 0, 0:un], AF.Exp, scale=-0.5), reads=["RS0"], writes=["RS1"])

            def p3(cs=range(8)):
                hres = Hres(t0, un)
                for c in cs:
                    tb = c % 2
                    S.op("dve", lambda e, c=c, tb=tb: e.scalar_tensor_tensor(
                        TMP[:, tb, 0:un], srcs[c], G[:, gidx(l, ni, c):gidx(l, ni, c) + 1], RS[:, 1, 0:un],
                        op0=ALU.mult, op1=ALU.mult), reads=src_res[c] + ["G", "RS1"], writes=[("TMP", tb)])
                    S.op("dve", lambda e, c=c, tb=tb: e.tensor_tensor(H[:, c, t0:t0 + un], H[:, c, t0:t0 + un], TMP[:, tb, 0:un], op=ALU.add),
                         reads=[("TMP", tb)] + hres, writes=hres)
            return [p1, p2, (lambda: p3(range(0, 4))), (lambda: p3(range(4, 8)))]

        def postnorm_unit(l, ni, t0, un, srcs, src_res):
            for t in post_tasks(l, ni, t0, un, srcs, src_res):
                t()

        def run_groups(l, ni, gsize, hal, body, after_first_pre=None):
            ng = T // gsize
            for gi in range(ng):
                o0 = gi * gsize
                o1 = o0 + gsize
                w0 = max(0, o0 - hal)
                w1 = min(T, o1 + hal)
                if gi > 0:
                    if hal > 0:
                        S.op("dve", lambda e: e.tensor_copy(XN[:, :, PAD:PAD + hal], XH[:, :, 0:hal]), reads=["XH"], writes=xnr(PAD, hal))
                    prenorm(l, ni, o0, w1 - o0, PAD + hal)
                else:
                    prenorm(l, ni, w0, w1 - w0, PAD)
                rc = PAD + (w1 - w0)
                S.op("dve", lambda e: e.memset(XN[:, :, rc:rc + PAD], 0.0), writes=xnr(rc, PAD))
                if gi == 0:
                    S.op("dve", lambda e: e.memset(XN[:, :, 0:PAD], 0.0), writes=xnr(0, PAD))
                if gi < ng - 1 and hal > 0:
                    cs = PAD + (o1 - hal - w0)
                    S.op("dve", lambda e: e.tensor_copy(XH[:, :, 0:hal], XN[:, :, cs:cs + hal]), reads=xnr(cs, hal), writes=["XH"])
                if gi == 0 and after_first_pre is not None:
                    after_first_pre()
                body(o0, gsize, w0, w1 - w0, PAD + (o0 - w0))

        def ffn(l):
            GS = 640
            NG = T // GS
            ycols = split_cols(GS + 2, 512)
            wi = lambda k, ch: CVF[:, (l * 3 + k) * 44 + ch:(l * 3 + k) * 44 + ch + 1]

            def window(g):
                o0 = g * GS
                w0 = max(0, o0 - 1)
                w1 = min(T, o0 + GS + 1)
                base = (g % 2) * 720
                return o0, w0, w1, base + PAD + (o0 - w0), base

            def pre(g):
                o0, w0, w1, oc0, base = window(g)
                head, units = [], []
                if g > 0:
                    po0, pw0, pw1, poc0, pbase = window(g - 1)
                    cs = poc0 + GS - 1
                    head.append(lambda: S.op("dve", lambda e: e.tensor_copy(XN[:, :, base + PAD:base + PAD + 1], XN[:, :, cs:cs + 1]), reads=xnr(cs, 1), writes=xnr(base + PAD, 1)))
                    units = pre_tasks(l, 4, o0, w1 - o0, base + PAD + 1, sq_eng="pool")
                else:
                    head.append(lambda: S.op("dve", lambda e: e.memset(XN[:, :, base:base + PAD], 0.0), writes=xnr(base, PAD)))
                    units = pre_tasks(l, 4, w0, w1 - w0, base + PAD)
                rc = base + PAD + (w1 - w0)
                head.append(lambda: S.op("dve", lambda e: e.memset(XN[:, :, rc:rc + PAD], 0.0), writes=xnr(rc, PAD)))
                return head, units

            def stageA(g, i, W, wres):
                o0, w0, w1, oc0, base = window(g)
                bb = i % 2
                ci = i % 2
                for which, Yb in ((0, YGb[bb]), (1, YUb[bb])):
                    for (y0, yn) in ycols:
                        pt, pres, _ = ps_full()
                        cc = (which * 2 + ci) * 128
                        mm(pt[:, 0:yn], pres, [(W[:, k, cc:cc + 128], XN[:, k, oc0 - 1 + y0:oc0 - 1 + y0 + yn]) for k in range(8)],
                           [wres] + xnr(oc0 - 1 + y0, yn))
                        S.op("act", lambda e, Yb=Yb, pt=pt, y0=y0, yn=yn: e.activation(Yb[:, 1 + y0:1 + y0 + yn], pt[:, 0:yn], AF.Copy),
                             reads=pres, writes=[("Y", which, bb)])

            def stageB(g, i):
                bb = i % 2
                YG, YU, CG = YGb[bb], YUb[bb], CGb[bb][:, 0:GS]
                rY0, rY1, rCG = ("Y", 0, bb), ("Y", 1, bb), ("CG", bb)
                S.op("dve", lambda e: e.tensor_scalar(CG, YG[:, 2:GS + 2], wi(1, i), None, op0=ALU.mult), reads=[rY0, "CVF"], writes=[rCG])
                S.op("dve", lambda e: e.scalar_tensor_tensor(CG, YG[:, 1:GS + 1], wi(0, i), CG, op0=ALU.mult, op1=ALU.add), reads=[rY0, rCG, "CVF"], writes=[rCG])
                S.op("dve", lambda e: e.scalar_tensor_tensor(CG, YG[:, 3:GS + 3], wi(2, i), CG, op0=ALU.mult, op1=ALU.add), reads=[rY0, rCG, "CVF"], writes=[rCG])
                S.op("act", lambda e: e.activation(CG, CG, AF.Silu), reads=[rCG], writes=[rCG])
                CU = YG[:, 2:GS + 2]
                CU = YG[:, 0:GS]
                S.op("dve", lambda e: e.tensor_scalar(CU, YU[:, 2:GS + 2], wi(1, 22 + i), None, op0=ALU.mult), reads=[rY1, rY0, "CVF"], writes=[rY0])
                S.op("dve", lambda e: e.scalar_tensor_tensor(CU, YU[:, 1:GS + 1], wi(0, 22 + i), CU, op0=ALU.mult, op1=ALU.add), reads=[rY1, rY0, "CVF"], writes=[rY0])
                S.op("dve", lambda e: e.scalar_tensor_tensor(CU, YU[:, 3:GS + 3], wi(2, 22 + i), CU, op0=ALU.mult, op1=ALU.add), reads=[rY1, rY0, "CVF"], writes=[rY0])
                S.op("dve", lambda e: e.tensor_tensor(HIDv[:, i, 0:GS], CG, CU, op=ALU.mult), reads=[rCG, rY0], writes=[("HID", i)])

            def post_units(g):
                o0 = g * GS
                tasks = []
                for (u0, un) in split_cols(GS, 256):
                    tasks += post_tasks(l, 5, o0 + u0, un, [TSv[:, c, u0:u0 + un] for c in range(8)], [[("TS", c)] for c in range(8)], sq_eng="pool")
                return tasks

            def pipelined(units):
                q = [units[k:k + 4] for k in range(0, len(units), 4)]
                if not q:
                    return []
                out = [q[0][0], q[0][1]]
                for u in range(1, len(q)):
                    out += [q[u][0], q[u - 1][2], q[u - 1][3], q[u][1]]
                out += [q[-1][2], q[-1][3]]
                return out

            h0, u0_ = pre(0)
            for t in h0 + u0_:
                t()
            S.fence([g_misc])
            pending = []
            post_u = []
            for g in range(NG):
                nxt_h, nxt_u = pre(g + 1) if g + 1 < NG else ([], [])
                pending = nxt_h + pipelined(post_u + nxt_u)
                W = wres = None
                for i in range(22):
                    if i % 2 == 0:
                        W, wres = wslot(f"f{l}_up{i // 2}")
                    stageA(g, i, W, wres)
                    if i > 0:
                        stageB(g, i - 1)
                    if pending and i >= 1:
                        for _ in range(-(-len(pending) // (22 - i))):
                            pending.pop(0)()
                stageB(g, 21)
                while pending:
                    pending.pop(0)()
                for oc in range(8):
                    W, wres = wslot(f"f{l}_dn{oc}")
                    for (c0, cn) in split_cols(GS, 512):
                        pt, pres, _ = ps_full()
                        mm(pt[:, 0:cn], pres, [(W[:, k, :], HIDv[:, k, c0:c0 + cn]) for k in range(22)], [wres], pair_reads=[[("HID", k)] for k in range(22)])
                        S.op("act", lambda e, pt=pt, c0=c0, cn=cn, oc=oc: e.activation(TSv[:, oc, c0:c0 + cn], pt[:, 0:cn], AF.Copy), reads=pres, writes=[("TS", oc)])
                post_u = post_units(g)
            for t in post_u:
                t()

        def xattn_setup(l, ti):
            S.dma(g_misc, STG[:, 0:2048], memt[ti].rearrange("(k p) m -> p k m", p=128), writes=["STG"])
            S.op("act", lambda e: e.activation(MEMB.rearrange("p k m -> p (k m)"), STG[:, 0:2048], AF.Copy), reads=["STG"], writes=["MEMB"])

        def xattn_kv(l):
            W, wres = wslot(f"x{l}_k")
            for h in range(4):
                pt, pres = ps_half()
                mm(pt, pres, [(W[:, k, h * 128:(h + 1) * 128], MEMB[:, k, :]) for k in range(8)], [wres, "MEMB"])
                S.op("act", lambda e, h=h, pt=pt: e.activation(KX[:, h, :], pt, AF.Copy), reads=pres, writes=["KX"])
            W, wres = wslot(f"x{l}_v")
            for mb in range(2):
                pt, pres, _ = ps_full()
                mm(pt, pres, [(MEMB[:, k, mb * 128:(mb + 1) * 128], W[:, k, :]) for k in range(8)], [wres, "MEMB"])
                S.op("act", lambda e, mb=mb, pt=pt: e.activation(VX[:, mb, :], pt, AF.Copy), reads=pres, writes=["VX"])


        def xattn(l):
            def body(o0, gs, w0, wl, oc0):
                subs = split_cols(gs, 512)
                Wq, wqr = wslot(f"x{l}_q")
                Wo, wor = wslot(f"x{l}_o", pin=(f"x{l}_q",))

                def stQ(si):
                    s0, sn = subs[si]
                    QX = QXb[si % 2]
                    for h in range(4):
                        pt, pres, _ = ps_full()
                        mm(pt[:, 0:sn], pres, [(Wq[:, k, h * 128:(h + 1) * 128], XN[:, k, oc0 + s0:oc0 + s0 + sn]) for k in range(8)], [wqr] + xnr(oc0 + s0, sn))
                        S.op("act", lambda e, h=h, pt=pt: e.activation(QX[:, h, 0:sn], pt[:, 0:sn], AF.Copy), reads=pres, writes=[("QX", si % 2, h)])

                def stA(si, h):
                    s0, sn = subs[si]
                    QX = QXb[si % 2]
                    pb = (si * 4 + h) % 2
                    PX = PXb[pb]
                    for mb in range(2):
                        pt, pres, _ = ps_full()
                        mm(pt[:, 0:sn], pres, [(KX[:, h, mb * 128:(mb + 1) * 128], QX[:, h, 0:sn])], ["KX", ("QX", si % 2, h)])
                        S.op("act", lambda e, mb=mb, pt=pt: e.activation(PX[:, mb, 0:sn], pt[:, 0:sn], AF.Exp, scale=float(128 ** -0.5)),
                             reads=pres, writes=[("PX", pb, mb)])

                def stB(si, h):
                    s0, sn = subs[si]
                    pb = (si * 4 + h) % 2
                    PX, RD, OX = PXb[pb], RDb[pb], OXb[si % 2]
                    pxr = [("PX", pb, 0), ("PX", pb, 1)]
                    po, pores, _ = ps_full()
                    mm(po[:, 0:sn], pores, [(VX[:, mb, h * 128:(h + 1) * 128], PX[:, mb, 0:sn]) for mb in range(2)], ["VX"] + pxr)
                    pd, pdres, _ = ps_full()
                    mm(pd[:, 0:sn], pdres, [(ONES[:], PX[:, mb, 0:sn]) for mb in range(2)], ["ONES"] + pxr)
                    S.op("act", lambda e: e.activation(RD[:, 0:sn], pd[:, 0:sn], AF.Ln), reads=pdres, writes=[("RD", pb)])
                    S.op("act", lambda e: e.activation(RD[:, 0:sn], RD[:, 0:sn], AF.Exp, scale=-1.0), reads=[("RD", pb)], writes=[("RD", pb)])
                    S.op("dve", lambda e: e.tensor_tensor(OX[:, h, 0:sn], po[:, 0:sn], RD[:, 0:sn], op=ALU.mult), reads=pores + [("RD", pb)], writes=[("OX", si % 2, h)])

                def stO(si):
                    s0, sn = subs[si]
                    OX = OXb[si % 2]
                    for (u0, un) in split_cols(sn, 256):
                        tl, tr = [], []
                        for oc in range(8):
                            pt, pres = ps_half()
                            mm(pt[:, 0:un], pres, [(Wo[:, h, oc * 128:(oc + 1) * 128], OX[:, h, u0:u0 + un]) for h in range(4)], [wor],
                               pair_reads=[[("OX", si % 2, h)] for h in range(4)])
                            tl.append(pt[:, 0:un]); tr.append(pres)
                        postnorm_unit(l, 3, o0 + s0 + u0, un, tl, tr)

                n = len(subs)
                stQ(0)
                for si in range(n):
                    stA(si, 0)
                    if si > 0:
                        stO(si - 1)
                    stA(si, 1); stB(si, 0)
                    stA(si, 2); stB(si, 1)
                    stA(si, 3); stB(si, 2)
                    if si + 1 < n:
                        stQ(si + 1)
                    stB(si, 3)
                stO(n - 1)

            def first():
                S.fence([g_misc], engines=("act", "dve", "pool", "sp"))
                xattn_setup(l, ti_cur[0])
                xattn_kv(l)
            run_groups(l, 2, 1280, 0, body, after_first_pre=first)


        def conformer(l):
            j = l // 2
            def body(o0, gs, w0, wl, oc0):
                S.op("dve", lambda e: e.memset(Ubuf, 0.0), reads=[("U", c) for c in range(8)], writes=[("U", c) for c in range(8)])
                for pj in range(4):
                    W, wres = wslot(f"cf{j}_pw1_{pj}")
                    for ci in range(2):
                        c = pj * 2 + ci
                        for (s0, sn) in split_cols(wl, 512):
                            pa, pares, _ = ps_full()
                            mm(pa[:, 0:sn], pares, [(W[:, k, ci * 128:(ci + 1) * 128], XN[:, k, PAD + s0:PAD + s0 + sn]) for k in range(8)], [wres] + xnr(PAD + s0, sn))
                            pg, pgres, _ = ps_full()
                            mm(pg[:, 0:sn], pgres, [(W[:, k, 256 + ci * 128:256 + (ci + 1) * 128], XN[:, k, PAD + s0:PAD + s0 + sn]) for k in range(8)], [wres] + xnr(PAD + s0, sn))
                            S.op("act", lambda e, pg=pg: e.activation(SG[:, 0:sn], pg[:, 0:sn], AF.Sigmoid), reads=pgres, writes=["SG"])
                            S.op("dve", lambda e, pa=pa, c=c, s0=s0, sn=sn: e.tensor_tensor(Ubuf[:, c, PAD + s0:PAD + s0 + sn], pa[:, 0:sn], SG[:, 0:sn], op=ALU.mult),
                                 reads=pares + ["SG"], writes=[("U", c)])
                for c in range(8):
                    W, wres = wslot(f"cf{j}_dg{c}")
                    for (s0, sn) in split_cols(gs, 512):
                        pt, pres, _ = ps_full()
                        mm(pt[:, 0:sn], pres, [(W[:, k, :], Ubuf[:, c, oc0 + s0 - 15 + k:oc0 + s0 - 15 + k + sn]) for k in range(31)], [wres, ("U", c)])
                        S.op("act", lambda e, pt=pt, c=c, s0=s0, sn=sn: e.activation(CO[:, c, s0:s0 + sn], pt[:, 0:sn], AF.Copy), reads=pres, writes=[("CO", c)])
                cor = [("CO", c) for c in range(8)]
                W0, w0r = wslot(f"cf{j}_pw2_0")
                W1, w1r = wslot(f"cf{j}_pw2_1", pin=(f"cf{j}_pw2_0",))

                def LN(u0, un):
                    S.op("act", lambda e: e.activation(SQ[:, :, 0:un], CO[:, :, u0:u0 + un], AF.Square), reads=cor, writes=["SQ"])
                    pm, pmres = ps_half()
                    mm(pm[:, 0:un], pmres, [(ONESD[:], CO[:, c, u0:u0 + un]) for c in range(8)], cor + ["ONESD"])
                    pe2, pe2res = ps_half()
                    mm(pe2[:, 0:un], pe2res, [(ONESD[:], SQ[:, c, 0:un]) for c in range(8)], ["SQ", "ONESD"])
                    S.op("act", lambda e: e.activation(LT[:, 0, 0:un], pm[:, 0:un], AF.Copy), reads=pmres, writes=["LT0"])
                    S.op("dve", lambda e: e.tensor_tensor(LT[:, 1, 0:un], LT[:, 0, 0:un], LT[:, 0, 0:un], op=ALU.mult), reads=["LT0"], writes=["LT1"])
                    S.op("dve", lambda e: e.tensor_tensor(LT[:, 1, 0:un], pe2[:, 0:un], LT[:, 1, 0:un], op=ALU.subtract), reads=pe2res + ["LT1"], writes=["LT1"])
                    S.op("act", lambda e: e.activation(LT[:, 2, 0:un], LT[:, 1, 0:un], AF.Ln, bias=EPSC[:, 0:1]), reads=["LT1", "EPSC"], writes=["LT2"])
                    S.op("act", lambda e: e.activation(LT[:, 3, 0:un], LT[:, 2, 0:un], AF.Exp, scale=-0.5), reads=["LT2"], writes=["LT3"])
                    for c in range(8):
                        db = c % 2
                        S.op("dve", lambda e, c=c, db=db: e.tensor_tensor(DT[:, db, 0:un], CO[:, c, u0:u0 + un], LT[:, 0, 0:un], op=ALU.subtract), reads=[("CO", c), "LT0"], writes=[("DT", db)])
                        S.op("dve", lambda e, db=db: e.tensor_tensor(DT[:, db, 0:un], DT[:, db, 0:un], LT[:, 3, 0:un], op=ALU.mult), reads=[("DT", db), "LT3"], writes=[("DT", db)])
                        S.op("act", lambda e, c=c, db=db: e.activation(XN[:, c, oc0 + u0:oc0 + u0 + un], DT[:, db, 0:un], AF.Silu,
                                                                       bias=LNB[:, j * 8 + c:j * 8 + c + 1], scale=LNG[:, j * 8 + c:j * 8 + c + 1]),
                             reads=[("DT", db), "LNG", "LNB"], writes=xnr(oc0 + u0, un))

                def PW(u0, un):
                    tl, tr = [], []
                    for oc in range(8):
                        Wp, wr_ = (W0, w0r) if oc < 4 else (W1, w1r)
                        pt, pres = ps_half()
                        mm(pt[:, 0:un], pres, [(Wp[:, k, (oc % 4) * 128:(oc % 4 + 1) * 128], XN[:, k, oc0 + u0:oc0 + u0 + un]) for k in range(8)], [wr_] + xnr(oc0 + u0, un))
                        tl.append(pt[:, 0:un]); tr.append(pres)
                    postnorm_unit(l, 1, o0 + u0, un, tl, tr)

                units = split_cols(gs, 256)
                for ui, (u0, un) in enumerate(units):
                    LN(u0, un)
                    if ui > 0:
                        PW(*units[ui - 1])
                PW(*units[-1])
            run_groups(l, 0, 1280, 16, body, after_first_pre=lambda: S.fence([g_misc]))

        NBW = 11

        def abmix(l):
            j = l // 2
            def body(o0, gs, w0, wl, oc0):
                if "0" in DBG:
                    return
                nbw = wl // 128
                ob0 = (o0 - w0) // 128
                S.op("dve", lambda e: e.memset(PC, 0.0), reads=[("PC", c) for c in range(4)], writes=[("PC", c) for c in range(4)])
                W, wres = wslot(f"ab{j}_kv")
                if "w" in DBG:
                    return
                for g in range(2):
                    for (s0, sn) in split_cols(wl, 512):
                        pt, pres, _ = ps_full()
                        mm(pt[:, 0:sn], pres, [(W[:, k, g * 128:(g + 1) * 128], XN[:, k, PAD + s0:PAD + s0 + sn]) for k in range(8)], [wres] + xnr(PAD + s0, sn))
                        S.op("act", lambda e, pt=pt, g=g, s0=s0, sn=sn: e.activation(KT[:, g, s0:s0 + sn], pt[:, 0:sn], AF.Copy), reads=pres, writes=[("KT", g)])
                if "k" in DBG:
                    return
                for b in range(nbw if "v" not in DBG else 1):
                    pt, pres = ps_half()
                    mm(pt[:, 0:128], pres, [(XN[:, k, PAD + b * 128:PAD + (b + 1) * 128], W[:, k, 256:384]) for k in range(8)], [wres] + xnr(PAD + b * 128, 128))
                    S.op("act", lambda e, pt=pt, b=b: e.activation(VV[:, b, :], pt[:, 0:128], AF.Copy), reads=pres, writes=["VV"])
                if "1" in DBG:
                    return
                W, wres = wslot(f"ab{j}_gc")
                for c in range(4):
                    for (s0, sn) in split_cols(wl, 512):
                        pt, pres, _ = ps_full()
                        mm(pt[:, 0:sn], pres, [(W[:, k, c * 128:(c + 1) * 128], XN[:, k, PAD + s0:PAD + s0 + sn]) for k in range(8)], [wres] + xnr(PAD + s0, sn))
                        S.op("act", lambda e, pt=pt, c=c, s0=s0, sn=sn: e.activation(PC[:, c, PAD + s0:PAD + s0 + sn], pt[:, 0:sn], AF.Copy), reads=pres, writes=[("PC", c)])
                W, wres = wslot(f"ab{j}_xa")
                for c in range(4):
                    for (s0, sn) in split_cols(wl, 512):
                        pt, pres, _ = ps_full()
                        mm(pt[:, 0:sn], pres, [(W[:, k, c * 128:(c + 1) * 128], XN[:, k, PAD + s0:PAD + s0 + sn]) for k in range(8)], [wres] + xnr(PAD + s0, sn))
                        S.op("dve", lambda e, pt=pt, c=c, s0=s0, sn=sn: e.tensor_tensor(PC[:, c, PAD + s0:PAD + s0 + sn], pt[:, 0:sn], PC[:, c, PAD + s0:PAD + s0 + sn], op=ALU.mult),
                             reads=pres + [("PC", c)], writes=[("PC", c)])
                for c in range(4):
                    cw = lambda k, c=c: CVA[:, (j * 3 + k) * 4 + c:(j * 3 + k) * 4 + c + 1]
                    for (s0, sn) in split_cols(gs, 512):
                        pc0 = oc0 + s0
                        S.op("dve", lambda e: e.tensor_scalar(GB[:, c, s0:s0 + sn], PC[:, c, pc0:pc0 + sn], cw(1), None, op0=ALU.mult), reads=[("PC", c), "CVA"], writes=[("gb", c)])
                        S.op("dve", lambda e: e.scalar_tensor_tensor(GB[:, c, s0:s0 + sn], PC[:, c, pc0 - 1:pc0 - 1 + sn], cw(0), GB[:, c, s0:s0 + sn], op0=ALU.mult, op1=ALU.add), reads=[("PC", c), "CVA", ("gb", c)], writes=[("gb", c)])
                        S.op("dve", lambda e: e.scalar_tensor_tensor(GB[:, c, s0:s0 + sn], PC[:, c, pc0 + 1:pc0 + 1 + sn], cw(2), GB[:, c, s0:s0 + sn], op0=ALU.mult, op1=ALU.add), reads=[("PC", c), "CVA", ("gb", c)], writes=[("gb", c)])
                W, wres = wslot(f"ab{j}_q")
                for c in range(4):
                    for (s0, sn) in split_cols(gs, 512):
                        pt, pres, _ = ps_full()
                        mm(pt[:, 0:sn], pres, [(W[:, k, c * 128:(c + 1) * 128], XN[:, k, oc0 + s0:oc0 + s0 + sn]) for k in range(8)], [wres] + xnr(oc0 + s0, sn))
                        S.op("act", lambda e, pt=pt, c=c, s0=s0, sn=sn: e.activation(QB[:, c, s0:s0 + sn], pt[:, 0:sn], AF.Copy), reads=pres, writes=[("q", c)])
                W, wres = wslot(f"ab{j}_gb")
                for c in range(4):
                    for (s0, sn) in split_cols(gs, 512):
                        pt, pres, _ = ps_full()
                        mm(pt[:, 0:sn], pres, [(W[:, k, c * 128:(c + 1) * 128], XN[:, k, oc0 + s0:oc0 + s0 + sn]) for k in range(8)], [wres] + xnr(oc0 + s0, sn))
                        S.op("dve", lambda e, pt=pt, c=c, s0=s0, sn=sn: e.tensor_tensor(GB[:, c, s0:s0 + sn], pt[:, 0:sn], GB[:, c, s0:s0 + sn], op=ALU.mult),
                             reads=pres + [("gb", c)], writes=[("gb", c)])
                if "2" in DBG:
                    return
                WA, war = wslot(f"ab{j}_woA")
                WB0, wb0r = wslot(f"ab{j}_woB0", pin=(f"ab{j}_woA",))
                WB1, wb1r = wslot(f"ab{j}_woB1", pin=(f"ab{j}_woA", f"ab{j}_woB0"))
                nq = gs // 128

                def stageS(qn):
                    qb = ob0 + qn
                    qc = qn * 128
                    pb = qn % 2
                    js = [jj for jj in (qb - 1, qb, qb + 1) if 0 <= jj < nbw]
                    for jj in js:
                        o = qb - jj + 1
                        js_ = jj - qb + 1
                        banks = [ps_full(), ps_full()]
                        for hp in range(2):
                            pt, pres, _ = banks[hp]
                            S.op("pe", lambda e, pt=pt, hp=hp, o=o: e.matmul(pt, lhsT=IDENT[:], rhs=EBB[:, hp, o, :, :].rearrange("p c q -> p (c q)"), start=True, stop=False),
                                 reads=["IDENT", "EB"], writes=pres, inc=False)
                        for h in range(8):
                            c, hp, gh = h // 2, h % 2, h // 4
                            pt, pres, _ = banks[hp]
                            S.op("pe", lambda e, pt=pt, c=c, hp=hp, gh=gh, jj=jj: e.matmul(
                                pt[:, c * 128:(c + 1) * 128], lhsT=KT[hp * 64:(hp + 1) * 64, gh, jj * 128:(jj + 1) * 128],
                                rhs=QB[hp * 64:(hp + 1) * 64, c, qc:qc + 128], start=False, stop=True),
                                reads=[("KT", 0), ("KT", 1)] + [("q", cc) for cc in range(4)] if h < 2 else (), writes=pres, inc=(h == 7))
                        for hp in range(2):
                            pt, pres, _ = banks[hp]
                            S.op("act", lambda e, pt=pt, hp=hp, js_=js_: e.activation(PT5[pb][:, js_, :, hp, :], pt.rearrange("p (c q) -> p c q", c=4), AF.Exp, scale=0.125),
                                 reads=pres, writes=[("PT", pb, js_)])

                def stageV(qn):
                    qb = ob0 + qn
                    pb = qn % 2
                    qi = qn % 2
                    js = [jj for jj in (qb - 1, qb, qb + 1) if 0 <= jj < nbw]
                    ptr = [("PT", pb, jj - qb + 1) for jj in js]
                    for gh in range(2):
                        po, pores, _ = ps_full()
                        mm(po[0:64, :], pores, [(VV[:, jj, gh * 64:(gh + 1) * 64], PTh[pb][:, jj - qb + 1, gh * 4:(gh + 1) * 4, :].rearrange("p h q -> p (h q)")) for jj in js],
                           ["VV"] + ptr)
                        pd, pdres, _ = ps_full()
                        prs = [(ONES[:, 0:64], PTh[pb][:, jj - qb + 1, gh * 4:(gh + 1) * 4, :].rearrange("p h q -> p (h q)")) for jj in js]
                        prs.append((ONES[0:1, 0:64], ESROW[0:1, (j * 2 + gh) * 512:(j * 2 + gh + 1) * 512]))
                        mm(pd[0:64, :], pdres, prs, ["ONES", "ESROW"] + ptr)
                        S.op("act", lambda e, pd=pd: e.activation(RDEN[0:64, :], pd[0:64, :], AF.Ln), reads=pdres, writes=[("PC", 0)])
                        S.op("act", lambda e: e.activation(RDEN[0:64, :], RDEN[0:64, :], AF.Exp, scale=-1.0), reads=[("PC", 0)], writes=[("PC", 0)])
                        S.op("dve", lambda e, po=po, gh=gh, qi=qi: e.tensor_tensor(
                            YB[0:64, gh * 4:(gh + 1) * 4, qi * 128:(qi + 1) * 128], po[0:64, :].rearrange("p (h q) -> p h q", h=4),
                            RDEN[0:64, :].rearrange("p (h q) -> p h q", h=4), op=ALU.mult), reads=pores + [("PC", 0)], writes=[("YB", qi)])

                def stageO(un_i):
                    s0 = un_i * 256
                    tl, tr = [], []
                    tiles, own = t_tiles()
                    for oc in range(8):
                        pt, pres = tiles[oc]
                        prs = [(WA[:, c, oc * 128:(oc + 1) * 128], GB[:, c, s0:s0 + 256]) for c in range(4)]
                        WBx = WB0 if oc < 4 else WB1
                        prs += [(WBx[:, h, (oc % 4) * 128:(oc % 4 + 1) * 128], YB[0:64, h, 0:256]) for h in range(8)]
                        mm(pt[:, 0:256], pres, prs, [war, wb0r, wb1r], pair_reads=[[("gb", c)] for c in range(4)] + [[("YB", 0), ("YB", 1)] for h in range(8)])
                        tl.append(pt[:, 0:256]); tr.append(pres)
                    tk = post_tasks(l, 1, o0 + s0, 256, tl, tr)
                    tk[0]()
                    return tk[1:] + [lambda: pinned.difference_update(own)]

                if "3" in DBG:
                    nq = 0
                rest = []
                for qn in range(nq):
                    stageS(qn)
                    for t in rest:
                        t()
                    rest = []
                    if qn > 0:
                        stageV(qn - 1)
                        if (qn - 1) % 2 == 1:
                            rest = stageO((qn - 1) // 2)
                if nq:
                    for t in rest:
                        t()
                    stageV(nq - 1)
                    for t in stageO((nq - 1) // 2):
                        t()
            run_groups(l, 0, 1280, 128, body, after_first_pre=lambda: S.fence([g_misc, g_cv] if first_ab[0] else [g_misc]))
            first_ab[0] = False

        ti_cur = [0]
        first_ab = [True]
        for ti in range(n_tiles):
            allH = [("H", b) for b in range(T // 128)]
            for q4 in range(4):
                c0_, c1_ = q4 * 640, (q4 + 1) * 640
                S.dma(g_ios[q4], H[:, :, c0_:c1_], xt[ti, :, c0_:c1_].rearrange("(c p) t -> p c t", p=128), writes=Hres(c0_, 640))
            sub = 0
            for l in range(depth):
                if sub >= n_sub:
                    break
                ti_cur[0] = ti
                if "c" in DBG:
                    pass
                elif l % 2 == 0:
                    abmix(l)
                else:
                    conformer(l)
                sub += 1
                if sub >= n_sub:
                    break
                if True:
                    xattn(l)
                sub += 1
                if sub >= n_sub:
                    break
                ffn(l)
                sub += 1
            for q4 in range(4):
                c0_, c1_ = q4 * 640, (q4 + 1) * 640
                S.dma(g_ios[q4], yt[ti, :, c0_:c1_].rearrange("(c p) t -> p c t", p=128), H[:, :, c0_:c1_], reads=Hres(c0_, 640), writes=[("yt", ti, q4)])
        S.wait_all("sp", [("yt", ti, q4) for ti in range(n_tiles) for q4 in range(4)])
    return nc


def tile_plan():
    tiles = []
    def plan(S_, n):
        return [int(round(i * (S_ - T) / (n - 1))) for i in range(n)]
    for s in plan(16384, 8):
        tiles.append(("p", 0, s, 16384))
    for b in range(4):
        for s in plan(8192, 4):
            tiles.append(("s", b, s, 8192))
    return tiles


def _t5_bucket(rel):
    half, max_exact = 16, 8
    ret = (rel > 0).astype(np.int32) * half
    n = np.abs(rel)
    nf = np.maximum(n, 1).astype(np.float32)
    large = max_exact + (np.log(nf / max_exact) / np.float32(np.log(128 / max_exact)) * (half - max_exact)).astype(np.int32)
    large = np.minimum(large, half - 1)
    return ret + np.where(n < max_exact, n, large)


def _fm(a, nchunk):
    lead = a.shape[:-1]
    r = a.reshape(*lead, nchunk, 128)
    r = np.moveaxis(r, -1, 0)
    return np.ascontiguousarray(r.reshape(128, -1))


def host_consts(inp):
    c = {}
    c["gains"] = _fm(np.asarray(inp["norm_g"], np.float32), 8)
    c["cva"] = _fm(np.asarray(inp["conv_a"], np.float32), 4)
    c["cvc"] = _fm(np.asarray(inp["conv_c"], np.float32), 8)
    c["lng"] = _fm(np.asarray(inp["ln_g_c"], np.float32), 8)
    c["lnb"] = _fm(np.asarray(inp["ln_b_c"], np.float32), 8)
    c["cvf"] = _fm(np.asarray(inp["conv_f"], np.float32), 44)
    sk = np.asarray(inp["sink_b"], np.float32)
    c["sinkrow"] = np.ascontiguousarray(np.repeat(sk.reshape(2, 2, 4, 1), 128, axis=3).reshape(1, 2048))
    jj = np.arange(128)[:, None]
    x = np.arange(384)[None, :]
    rel = jj - (x % 128) - (x // 128 - 1) * 128
    bidx = _t5_bucket(rel)
    rb = np.asarray(inp["rel_bias"], np.float32)
    bt = rb[bidx]
    c["biasT"] = np.ascontiguousarray(np.transpose(bt, (0, 2, 1)).reshape(128, 8 * 384))
    c["bmask"] = (np.abs(rel) <= 128).astype(np.float32)
    c["ident"] = np.eye(128, dtype=np.float32)
    return c


_NC_CACHE = {}


def kernel(**inp):
    tiles = tile_plan()
    xp = np.asarray(inp["x_prompt"], np.float32)
    xs = np.asarray(inp["x_sample"], np.float32)
    mp = np.asarray(inp["mem_prompt"], np.float32)
    ms = np.asarray(inp["mem_sample"], np.float32)
    consts = host_consts(inp)
    wnames = ["w_in_ab", "w_out_ab", "w_pw1_c", "w_pw2_c", "w_xq", "w_xkv", "w_xo", "w_up", "w_down"]
    weights = {k: np.ascontiguousarray(np.asarray(inp[k], np.float32)) for k in wnames}
    in_maps = []
    for core in range(NCORES):
        xt = np.empty((NTC, D, T), np.float32)
        mt = np.empty((NTC, D, 256), np.float32)
        for i in range(NTC):
            grp, b, s, _ = tiles[core * NTC + i]
            src = xp if grp == "p" else xs
            mem = mp if grp == "p" else ms
            xt[i] = src[b, s:s + T, :].T
            mt[i] = mem[b].T
        m = {"xt": xt, "memt": mt}
        m.update(weights)
        m.update(consts)
        in_maps.append(m)
    if "nc" not in _NC_CACHE:
        rec = []
        build_program(n_sub=12, record=rec)
        tg = {}
        build_program(n_sub=12, wseq=rec, inc_record=tg)
        _NC_CACHE["nc"] = build_program(n_sub=12, wseq=rec, inc_targets=tg)
    res = run_bass_kernel_spmd(_NC_CACHE["nc"], in_maps, core_ids=list(range(NCORES)))
    yp = np.empty_like(xp)
    ys = np.empty_like(xs)
    for core in range(NCORES):
        yt = res.results[core]["yt"]
        for i in range(NTC):
            grp, b, s, L = tiles[core * NTC + i]
            lo = 0 if s == 0 else HALO
            hi = T if s + T == L else T - HALO
            dst = yp if grp == "p" else ys
            dst[b, s + lo:s + hi, :] = yt[i][:, lo:hi].T
    return (yp, ys)
```

```python
import numpy as np
from contextlib import ExitStack
import concourse.bass as bass
import concourse.mybir as mybir
from concourse.bass_utils import run_bass_kernel_spmd

F32 = mybir.dt.float32
BF16 = mybir.dt.bfloat16
AF = mybir.ActivationFunctionType
ALU = mybir.AluOpType

D = 1024
T = 2560
NTC = 3
HALO = 290
PAD = 16
XW = PAD + 1280 + 128 + PAD
DEPTH = 4
EPS = 1e-6
DFF = 2816
NCORES = 8


class Sched:
    LIMIT = 30000 - 30000 % 16

    def __init__(self, nc, stack, inc_record=None, inc_targets=None):
        self.nc = nc
        self.stack = stack
        self.inc_record = inc_record
        self.inc_targets = inc_targets
        self.cand = {}
        self.engs = {"pe": nc.tensor, "act": nc.scalar, "dve": nc.vector, "pool": nc.gpsimd, "sp": nc.sync}
        self.sem, self.cnt, self.epoch, self.step = {}, {}, {}, {}
        self.waited = {e: {} for e in self.engs}
        self.last_write = {}
        self.readers = {}
        self.nsem = 0
        for e in self.engs:
            self._newprod(e, 1)

    def _newsem(self, name):
        self.nsem += 1
        return self.stack.enter_context(self.nc.semaphore(f"s{self.nsem}_{name}"))

    def _newprod(self, p, step):
        self.sem[p] = [self._newsem(p)]
        self.cnt[p] = 0
        self.epoch[p] = 0
        self.step[p] = step

    def dma_group(self, name):
        self._newprod(name, 16)
        return name

    def _deps(self, e, reads, writes):
        deps = {}

        def add(t):
            p, ep, c = t
            cur = deps.get(p)
            if cur is None or (ep, c) > cur:
                deps[p] = (ep, c)

        for r in reads:
            lw = self.last_write.get(r)
            if lw is not None and not (e == "pe" and lw[0] == "pe"):
                add(lw)
        for w in writes:
            lw = self.last_write.get(w)
            if lw is not None and lw[0] != e:
                add(lw)
            for rd in self.readers.get(w, ()):
                if rd[0] != e:
                    add(rd)
        return deps

    def _emit_waits(self, e, deps):
        eng = self.engs[e]
        for p, (ep, c) in deps.items():
            cur = self.waited[e].get(p)
            if cur is not None and cur >= (ep, c):
                continue
            if self.inc_record is not None and p in self.engs:
                self.inc_record.setdefault(p, set()).add(ep * self.LIMIT + c)
            if self.inc_targets is not None and p in self.engs:
                assert (ep, c) <= (self.epoch[p], self.cnt[p]) or p == e, ("wait on un-emitted increment", e, p, ep, c)
            eng.wait_ge(self.sem[p][ep], c)
            self.waited[e][p] = (ep, c)

    def _commit(self, p, ins, inc):
        if inc:
            ins.then_inc(self.sem[p][self.epoch[p]], self.step[p])
            self.cnt[p] += self.step[p]
            if self.cnt[p] >= self.LIMIT:
                self.sem[p].append(self._newsem(p))
                self.epoch[p] += 1
                self.cnt[p] = 0

    def _record(self, t, reads, writes):
        for r in reads:
            lst = self.readers.setdefault(r, [])
            lst[:] = [x for x in lst if x[0] != t[0]]
            lst.append(t)
        for w in writes:
            self.last_write[w] = t
            self.readers[w] = []

    def op(self, e, fn, reads=(), writes=(), inc=True):
        deps = self._deps(e, reads, writes)
        self._emit_waits(e, deps)
        t = (e, self.epoch[e], self.cnt[e] + 1)
        if inc and self.inc_targets is not None:
            k = self.cand.get(e, 0) + 1
            self.cand[e] = k
            if k not in self.inc_targets.get(e, ()):
                inc = False
        ins = fn(self.engs[e])
        self._commit(e, ins, inc)
        self._record(t, reads, writes)
        return t

    def dma(self, group, out, in_, reads=(), writes=(), e="sp"):
        deps = self._deps(group, reads, writes)
        self._emit_waits(e, deps)
        t = (group, self.epoch[group], self.cnt[group] + 16)
        ins = self.engs[e].dma_start(out=out, in_=in_)
        self._commit(group, ins, True)
        self._record(t, reads, writes)
        return t

    def fence(self, extra=(), engines=("act", "dve", "pool")):
        prods = ["pe", "act", "dve", "pool"] + list(extra)
        for e in engines:
            deps = {}
            for p in prods:
                if p == e:
                    continue
                ep, c = self.epoch[p], self.cnt[p]
                if c == 0 and ep > 0:
                    ep, c = ep - 1, self.LIMIT
                if c > 0:
                    deps[p] = (ep, c)
            self._emit_waits(e, deps)

    def wait_all(self, e, resources):
        deps = {}
        for r in resources:
            lw = self.last_write.get(r)
            if lw is not None:
                p, ep, c = lw
                if p not in deps or (ep, c) > deps[p]:
                    deps[p] = (ep, c)
        self._emit_waits(e, deps)


def split_cols(n, mx=512):
    out, c = [], 0
    while c < n:
        w = min(mx, n - c)
        out.append((c, w))
        c += w
    return out


def blocks_of(t0, n):
    return range(t0 // 128, (t0 + n - 1) // 128 + 1)


def build_program(n_sub=12 * 1, n_tiles=NTC, depth=DEPTH, wseq=None, record=None, inc_record=None, inc_targets=None):
    nc = bass.Bass("TRN2", target_bir_lowering=False)
    dr = lambda n, s, d, k: nc.dram_tensor(n, s, d, kind=k)
    xt = dr("xt", [NTC, D, T], F32, "ExternalInput").ap()
    memt = dr("memt", [NTC, D, 256], F32, "ExternalInput").ap()
    yt = dr("yt", [NTC, D, T], F32, "ExternalOutput").ap()
    w32 = {}
    wshapes = {"w_in_ab": [2, D, 2304], "w_out_ab": [2, D, D], "w_pw1_c": [2, D, 2 * D], "w_pw2_c": [2, D, D],
               "w_xq": [4, D, 512], "w_xkv": [4, D, D], "w_xo": [4, 512, D], "w_up": [4, D, 2 * DFF],
               "w_down": [4, DFF, D]}
    for k, s in wshapes.items():
        w32[k] = dr(k, s, F32, "ExternalInput").ap()
    cshapes = {"gains": [128, 4 * 6 * 8], "cva": [128, 2 * 3 * 4], "cvc": [128, 2 * 31 * 8], "lng": [128, 16],
               "lnb": [128, 16], "cvf": [128, 4 * 3 * 44], "sinkrow": [1, 2 * 2 * 512], "biasT": [128, 8 * 384],
               "bmask": [128, 384], "ident": [128, 128]}
    cin = {k: dr(k, s, F32, "ExternalInput").ap() for k, s in cshapes.items()}

    pieces = {}

    def add_piece(pid, parts, kc, cols, srcs):
        t = dr("sc_" + pid, [parts, kc * cols], BF16, "Internal").ap()
        pieces[pid] = dict(dram=t, parts=parts, kc=kc, cols=cols, srcs=srcs)

    def kview(w, l, r0, nrows, c0, c1, p=128):
        return w[l, r0:r0 + nrows, c0:c1].rearrange("(k p) n -> p k n", p=p)

    for j in range(2):
        wi = w32["w_in_ab"]
        for nm, c0 in (("gb", 0), ("gc", 512), ("xa", 1024), ("q", 1536)):
            add_piece(f"ab{j}_{nm}", 128, 8, 512, [((0, 512), kview(wi, j, 0, D, c0, c0 + 512))])
        add_piece(f"ab{j}_kv", 128, 8, 384, [
            ((0, 64), kview(wi, j, 0, D, 2048, 2112)), ((64, 128), kview(wi, j, 0, D, 2048, 2112)),
            ((128, 192), kview(wi, j, 0, D, 2112, 2176)), ((192, 256), kview(wi, j, 0, D, 2112, 2176)),
            ((256, 384), kview(wi, j, 0, D, 2176, 2304))])
        wo = w32["w_out_ab"]
        add_piece(f"ab{j}_woA", 128, 4, 1024, [((0, 1024), kview(wo, j, 0, 512, 0, 1024))])
        for hh in range(2):
            add_piece(f"ab{j}_woB{hh}", 64, 8, 512,
                      [((0, 512), wo[j, 512:1024, hh * 512:(hh + 1) * 512].rearrange("(h d) n -> d h n", d=64))])
        w1 = w32["w_pw1_c"]
        for pj in range(4):
            add_piece(f"cf{j}_pw1_{pj}", 128, 8, 512, [
                ((0, 256), kview(w1, j, 0, D, pj * 256, pj * 256 + 256)),
                ((256, 512), kview(w1, j, 0, D, 1024 + pj * 256, 1024 + pj * 256 + 256))])
        for pj in range(2):
            add_piece(f"cf{j}_pw2_{pj}", 128, 8, 512, [((0, 512), kview(w32["w_pw2_c"], j, 0, D, pj * 512, pj * 512 + 512))])
        for c in range(8):
            add_piece(f"cf{j}_dg{c}", 128, 31, 128, [])
    for l in range(4):
        add_piece(f"x{l}_q", 128, 8, 512, [((0, 512), kview(w32["w_xq"], l, 0, D, 0, 512))])
        add_piece(f"x{l}_k", 128, 8, 512, [((0, 512), kview(w32["w_xkv"], l, 0, D, 0, 512))])
        add_piece(f"x{l}_v", 128, 8, 512, [((0, 512), kview(w32["w_xkv"], l, 0, D, 512, 1024))])
        add_piece(f"x{l}_o", 128, 4, 1024, [((0, 1024), kview(w32["w_xo"], l, 0, 512, 0, 1024))])
        wu = w32["w_up"]
        for pj in range(11):
            add_piece(f"f{l}_up{pj}", 128, 8, 512, [
                ((0, 256), kview(wu, l, 0, D, pj * 256, pj * 256 + 256)),
                ((256, 512), kview(wu, l, 0, D, DFF + pj * 256, DFF + pj * 256 + 256))])
        for oc in range(8):
            add_piece(f"f{l}_dn{oc}", 128, 22, 128, [((0, 128), kview(w32["w_down"], l, 0, DFF, oc * 128, oc * 128 + 128))])

    with ExitStack() as st:
        S = Sched(nc, st, inc_record=inc_record, inc_targets=inc_targets)
        sb = lambda n, s, d: st.enter_context(nc.sbuf_tensor(n, s, d))
        PS = st.enter_context(nc.psum_tensor("PS", [128, 8, 512], F32))
        H = sb("H", [128, 8, T], F32)
        XN = sb("XN", [128, 8, XW], BF16)
        XH = sb("XH", [128, 8, 128], BF16)
        SLOTS = [sb(f"slot{i}", [128, 4096], BF16) for i in range(3)]
        G = sb("G", [128, 192], F32)
        CVA = sb("CVA", [128, 24], F32)
        CVC = sb("CVC", [128, 496], F32)
        LNG = sb("LNG", [128, 16], F32)
        LNB = sb("LNB", [128, 16], F32)
        CVF = sb("CVF", [128, 528], F32)
        ESROW = sb("ESROW", [1, 2048], BF16)
        EB = sb("EB", [128, 8, 384], BF16)
        ONES = sb("ONES", [128, 128], BF16)
        ONESD = sb("ONESD", [128, 128], BF16)
        IDENT = sb("IDENT", [128, 128], BF16)
        EPSC = sb("EPSC", [128, 1], F32)
        SQ = sb("SQ", [128, 8, 256], BF16)
        RS = sb("RS", [128, 2, 256], F32)
        TMP = sb("TMP", [128, 2, 256], F32)
        reg_base = 229344 - nc.sbuf_bytes_remaining
        reg_base = (reg_base + 31) // 32 * 32
        REG = sb("REG", [128, 14208 + 16], F32)
        REGF = nc.alloc_sbuf_tensor_at("REGF", [128, 14208], F32, offset=reg_base)
        REGB = nc.alloc_sbuf_tensor_at("REGB", [128, 28416], BF16, offset=reg_base)

        def rv(off, n, dt):
            if dt == F32:
                return REGF[:, off // 4:off // 4 + n]
            return REGB[:, off // 2:off // 2 + n]

        HIDv = rv(0, 22 * 640, BF16).rearrange("p (c n) -> p c n", c=22)
        TSv = rv(28160, 8 * 640, F32).rearrange("p (c n) -> p c n", c=8)
        YGb = [rv(48640, 648, BF16), rv(52528, 648, BF16)]
        YUb = [rv(49936, 648, BF16), rv(53824, 648, BF16)]
        CGb = [rv(51232, 648, BF16), rv(55120, 648, BF16)]
        Ubuf = rv(0, 8 * 1344, BF16).rearrange("p (c n) -> p c n", c=8)
        CO = rv(21504, 8 * 1280, BF16).rearrange("p (c n) -> p c n", c=8)
        SG = rv(41984, 512, F32)
        LT = rv(44032, 1024, F32).rearrange("p (a n) -> p a n", a=4)
        DT = rv(48128, 512, F32).rearrange("p (a n) -> p a n", a=2)
        SQL = rv(50176, 8 * 256, BF16).rearrange("p (c n) -> p c n", c=8)
        PC = rv(0, 4 * 1440, BF16).rearrange("p (c n) -> p c n", c=4)
        KT = rv(11520, 2 * 1408, BF16).rearrange("p (g n) -> p g n", g=2)
        QB = rv(17152, 4 * 1280, BF16).rearrange("p (c n) -> p c n", c=4)
        GB = rv(27392, 4 * 1280, BF16).rearrange("p (c n) -> p c n", c=4)
        PT5 = [rv(37632 + 6144 * b, 3 * 8 * 128, BF16).rearrange("p (j c t q) -> p j c t q", j=3, c=4, t=2) for b in range(2)]
        PTh = [rv(37632 + 6144 * b, 3 * 8 * 128, BF16).rearrange("p (j h q) -> p j h q", j=3, h=8) for b in range(2)]
        YB = rv(49920, 8 * 256, BF16).rearrange("p (h n) -> p h n", h=8)
        VV = rv(54016, 11 * 128, BF16).rearrange("p (b n) -> p b n", b=11)
        RDEN = rv(0, 512, F32)
        QXb = [rv(4096 * b, 4 * 512, BF16).rearrange("p (h n) -> p h n", h=4) for b in range(2)]
        PXb = [rv(8192 + 2048 * b, 2 * 512, BF16).rearrange("p (m n) -> p m n", m=2) for b in range(2)]
        OXb = [rv(12288 + 4096 * b, 4 * 512, BF16).rearrange("p (h n) -> p h n", h=4) for b in range(2)]
        RDb = [rv(20480 + 2048 * b, 512, F32) for b in range(2)]
        MEMB = rv(24576, 8 * 256, BF16).rearrange("p (k m) -> p k m", k=8)
        KX = rv(28672, 4 * 256, BF16).rearrange("p (h m) -> p h m", h=4)
        VX = rv(30720, 2 * 512, BF16).rearrange("p (m n) -> p m n", m=2)
        STG = rv(32768, 2048, F32)
        DGSf = rv(0, 31 * 128, BF16)
        BT = rv(8192, 3072, F32)

        EBB = EB[:].rearrange("p h x -> p (h x)").rearrange("p (t o c q) -> p t o c q", t=2, o=3, c=4)
        gq = [S.dma_group(f"gw{i}") for i in range(3)]
        g_misc = S.dma_group("gmisc")
        g_cv = S.dma_group("gcv")
        g_cvs = [S.dma_group("gcv0"), S.dma_group("gcv1")]
        cv_state = {"k": 0, "last": [None, None], "dg": None}
        g_io = S.dma_group("gio")
        g_ios = [S.dma_group(f"gio{q}") for q in range(4)]

        psc = {"f": 0, "h": 0}

        pinned = set()

        def ps_full():
            while True:
                b = psc["f"] % 8
                psc["f"] += 1
                if b not in pinned:
                    break
            return PS[:, b, :], [("ps", b)], b

        def ps_half(allow=None):
            while True:
                i = psc["h"] % 16
                psc["h"] += 1
                b, hf = i // 2, i % 2
                if b not in pinned or (allow is not None and b in allow):
                    break
            return PS[:, b, hf * 256:(hf + 1) * 256], [("ps", b)]

        def t_tiles():
            own = set()
            tiles = []
            while len(tiles) < 8:
                i = psc["h"] % 16
                b, hf = i // 2, i % 2
                if hf == 1 and b not in own:
                    psc["h"] += 1
                    continue
                if b in pinned and b not in own:
                    psc["h"] += 1
                    continue
                psc["h"] += 1
                own.add(b)
                pinned.add(b)
                tiles.append((PS[:, b, hf * 256:(hf + 1) * 256], [("ps", b)]))
            return tiles, own

        def mm(out_ap, out_res, pairs, reads, pair_reads=None):
            n = len(pairs)
            for i, (l, r) in enumerate(pairs):
                rd = list(reads) if i == 0 else []
                if pair_reads is not None:
                    rd += list(pair_reads[i])
                S.op("pe", lambda e, l=l, r=r, i=i: e.matmul(out_ap, lhsT=l, rhs=r, start=(i == 0), stop=(i == n - 1)),
                     reads=rd, writes=out_res, inc=(i == n - 1))

        def xnr(c0, n):
            return [("XN", b) for b in range(c0 // 16, (c0 + n - 1) // 16 + 1)]

        def ld(dst, src, res):
            S.dma(g_misc, dst, src, writes=[res])

        import os
        DBG = os.environ.get("KDBG", "")
        ld(G[:], cin["gains"], "G"); ld(CVA[:], cin["cva"], "CVA"); ld(CVC[:], cin["cvc"], "CVC")
        ld(LNG[:], cin["lng"], "LNG"); ld(LNB[:], cin["lnb"], "LNB"); ld(CVF[:], cin["cvf"], "CVF")
        S.wait_all("sp", ["G", "CVA", "CVC", "LNG", "LNB", "CVF"])
        S.op("dve", lambda e: e.memset(EPSC[:], EPS), writes=["EPSC"])
        S.op("dve", lambda e: e.memset(ONES[:], 1.0), writes=["ONES"])
        S.op("dve", lambda e: e.memset(ONESD[:], 1.0 / D), writes=["ONESD"])
        S.op("dve", lambda e: e.memset(XN[:], 0.0), writes=xnr(0, XW))
        ld(STG[:, 0:128], cin["ident"], "STG")
        S.op("dve", lambda e: e.tensor_copy(IDENT[:], STG[:, 0:128]), reads=["STG"], writes=["IDENT"])
        if "a" not in DBG:
            ld(STG[0:1, 0:2048], cin["sinkrow"], "STG")
            S.op("act", lambda e: e.activation(ESROW[:], STG[0:1, 0:2048], AF.Exp), reads=["STG"], writes=["ESROW"])
        ld(BT, cin["biasT"], "R3")
        ld(STG[:, 0:384], cin["bmask"], "STG")
        NEGT = STG[:, 512:896]
        S.op("dve", lambda e: e.tensor_scalar(NEGT, STG[:, 0:384], 240000.0, -240000.0, op0=ALU.mult, op1=ALU.add), reads=["STG"], writes=["NEGT"])
        for h in range(8):
            S.op("dve", lambda e, h=h: e.scalar_tensor_tensor(BT[:, h * 384:(h + 1) * 384], BT[:, h * 384:(h + 1) * 384], 8.0, STG[:, 0:384], op0=ALU.mult, op1=ALU.mult),
                 reads=["R3", "STG"], writes=["R3"])
            S.op("dve", lambda e, h=h: e.tensor_tensor(EBB[:, h % 2, :, h // 2, :], BT[:, h * 384:(h + 1) * 384].rearrange("p (o q) -> p o q", o=3),
                                                       NEGT.rearrange("p (o q) -> p o q", o=3), op=ALU.add),
                 reads=["R3", "NEGT"], writes=["EB"])

        def convert(pid):
            p = pieces[pid]
            dv = p["dram"].rearrange("p (k n) -> p k n", k=p["kc"])
            for (c0, c1), src in p["srcs"]:
                gi_ = cv_state["k"] % 2
                cv_state["k"] += 1
                lt = cv_state["last"][gi_]
                if lt is not None:
                    S._emit_waits("pool", {lt[0]: (lt[1], lt[2])})
                cv_state["last"][gi_] = S.dma(g_cvs[gi_], dv[:, :, c0:c1], src, reads=[("sc", pid)] if len(p["srcs"]) > 1 else (), writes=[("sc", pid)], e="pool")


        DGS = DGSf.rearrange("p (k n) -> p k n", k=31)

        def build_diag(j, c):
            for k in range(31):
                S.op("dve", lambda e, k=k: e.tensor_scalar(DGS[:, k, :], IDENT[:], CVC[:, (j * 31 + k) * 8 + c:(j * 31 + k) * 8 + c + 1],
                                                           None, op0=ALU.mult), reads=["IDENT", "CVC"], writes=["DGS"])
            if cv_state["dg"] is not None:
                lt = cv_state["dg"]
                S._emit_waits("sp", {lt[0]: (lt[1], lt[2])})
            cv_state["dg"] = S.dma(g_cv, pieces[f"cf{j}_dg{c}"]["dram"], DGSf, reads=["DGS"], writes=[("sc", f"cf{j}_dg{c}")])

        def layer_pids(l):
            j = l // 2
            out = []
            if l % 2 == 0:
                out += [f"ab{j}_{n}" for n in ("kv", "gc", "xa", "gb", "q", "woA", "woB0", "woB1")]
            else:
                out += [f"cf{j}_pw1_{i}" for i in range(4)] + [f"cf{j}_pw2_{i}" for i in range(2)]
            out += [f"x{l}_k", f"x{l}_v", f"x{l}_q", f"x{l}_o"]
            out += [f"f{l}_up{i}" for i in range(11)] + [f"f{l}_dn{i}" for i in range(8)]
            return out

        nlay = min(depth, (n_sub + 2) // 3)
        for l in range(nlay):
            for pid in layer_pids(l):
                convert(pid)
            if l % 2 == 1:
                for c in range(8):
                    build_diag(l // 2, c)

        slot_pid = [None, None, None]
        slot_use = [0, 0, 0]
        clock = [0]

        wptr = [0]

        def _load(pid, i):
            p = pieces[pid]
            n = p["kc"] * p["cols"]
            S.dma(gq[i], SLOTS[i][0:p["parts"], 0:n], p["dram"], reads=[("sc", pid)], writes=[("slot", i)])
            slot_pid[i] = pid

        def wslot(pid, pin=()):
            clock[0] += 1
            if record is not None:
                record.append((pid, tuple(pin)))
            if pid in slot_pid:
                i = slot_pid.index(pid)
            else:
                cands = [i for i in range(3) if slot_pid[i] not in pin]
                i = min(cands, key=lambda i: slot_use[i])
                _load(pid, i)
            slot_use[i] = clock[0]
            cur = wptr[0]
            wptr[0] += 1
            if wseq is not None:
                assert wseq[cur][0] == pid, (cur, wseq[cur], pid)
                keep = {pid} | set(pin)
                m = cur + 1
                while m < len(wseq) and m < cur + 40:
                    npid, npin = wseq[m]
                    if npid in slot_pid:
                        keep.add(npid)
                        keep |= set(npin)
                        m += 1
                        continue
                    cands = [k for k in range(3) if slot_pid[k] not in keep]
                    if not cands:
                        break
                    k = min(cands, key=lambda k: slot_use[k])
                    _load(npid, k)
                    slot_use[k] = clock[0]
                    keep.add(npid)
                    keep |= set(npin)
                    m += 1
            p = pieces[pid]
            v = SLOTS[i][0:p["parts"], 0:p["kc"] * p["cols"]].rearrange("p (k n) -> p k n", k=p["kc"])
            return v, ("slot", i)

        def gidx(l, ni, c):
            return (l * 6 + ni) * 8 + c

        def Hres(t0, n):
            return [("H", b) for b in blocks_of(t0, n)]

        def pre_tasks(l, ni, t0, n, col0, sq_eng="act"):
            tasks = []
            for (u0, un) in split_cols(n, 256):
                a = t0 + u0
                hres = Hres(a, un)
                st = {}

                def p1(a=a, un=un, hres=hres):
                    if sq_eng == "pool":
                        S.op("pool", lambda e: e.tensor_tensor(SQ[:, :, 0:un], H[:, :, a:a + un], H[:, :, a:a + un], op=ALU.mult), reads=hres, writes=["SQ"])
                    else:
                        S.op("act", lambda e: e.activation(SQ[:, :, 0:un], H[:, :, a:a + un], AF.Square), reads=hres, writes=["SQ"])

                def p2(un=un, st=st):
                    pm, pres = ps_half()
                    mm(pm[:, 0:un], pres, [(ONESD[:], SQ[:, c, 0:un]) for c in range(8)], ["SQ", "ONESD"])
                    S.op("act", lambda e: e.activation(RS[:, 0, 0:un], pm[:, 0:un], AF.Ln, bias=EPSC[:, 0:1]), reads=pres + ["EPSC"], writes=["RS0"])
                    S.op("act", lambda e: e.activation(RS[:, 1, 0:un], RS[:, 0, 0:un], AF.Exp, scale=-0.5), reads=["RS0"], writes=["RS1"])

                def p3(a=a, un=un, u0=u0, hres=hres, cs=range(8)):
                    for c in cs:
                        S.op("dve", lambda e, c=c: e.scalar_tensor_tensor(
                            XN[:, c, col0 + u0:col0 + u0 + un], H[:, c, a:a + un], G[:, gidx(l, ni, c):gidx(l, ni, c) + 1],
                            RS[:, 1, 0:un], op0=ALU.mult, op1=ALU.mult), reads=hres + ["G", "RS1"], writes=xnr(col0 + u0, un))
                tasks += [p1, p2, (lambda p3=p3: p3(cs=range(0, 4))), (lambda p3=p3: p3(cs=range(4, 8)))]
            return tasks

        def prenorm(l, ni, t0, n, col0):
            for t in pre_tasks(l, ni, t0, n, col0):
                t()

        def post_tasks(l, ni, t0, un, srcs, src_res, sq_eng="act"):
            def p1():
                for c in range(8):
                    if sq_eng == "pool":
                        S.op("pool", lambda e, c=c: e.tensor_tensor(SQ[:, c, 0:un], srcs[c], srcs[c], op=ALU.mult), reads=src_res[c], writes=["SQ"])
                    else:
                        S.op("act", lambda e, c=c: e.activation(SQ[:, c, 0:un], srcs[c], AF.Square), reads=src_res[c], writes=["SQ"])

            def p2():
                pm, pres = ps_half()
                mm(pm[:, 0:un], pres, [(ONESD[:], SQ[:, c, 0:un]) for c in range(8)], ["SQ", "ONESD"])
                S.op("act", lambda e: e.activation(RS[:, 0, 0:un], pm[:, 0:un], AF.Ln, bias=EPSC[:, 0:1]), reads=pres + ["EPSC"], writes=["RS0"])
                S.op("act", lambda e: e.activation(RS[:, 1, 0:un], RS[:, 0, 0:un], AF.Exp, scale=-0.5), reads=["RS0"], writes=["RS1"])

            def p3(cs=range(8)):
                hres = Hres(t0, un)
                for c in cs:
                    tb = c % 2
                    S.op("dve", lambda e, c=c, tb=tb: e.scalar_tensor_tensor(
                        TMP[:, tb, 0:un], srcs[c], G[:, gidx(l, ni, c):gidx(l, ni, c) + 1], RS[:, 1, 0:un],
                        op0=ALU.mult, op1=ALU.mult), reads=src_res[c] + ["G", "RS1"], writes=[("TMP", tb)])
                    S.op("dve", lambda e, c=c, tb=tb: e.tensor_tensor(H[:, c, t0:t0 + un], H[:, c, t0:t0 + un], TMP[:, tb, 0:un], op=ALU.add),
                         reads=[("TMP", tb)] + hres, writes=hres)
            return [p1, p2, (lambda: p3(range(0, 4))), (lambda: p3(range(4, 8)))]

        def postnorm_unit(l, ni, t0, un, srcs, src_res):
            for t in post_tasks(l, ni, t0, un, srcs, src_res):
                t()

        def run_groups(l, ni, gsize, hal, body, after_first_pre=None):
            ng = T // gsize
            for gi in range(ng):
                o0 = gi * gsize
                o1 = o0 + gsize
                w0 = max(0, o0 - hal)
                w1 = min(T, o1 + hal)
                if gi > 0:
                    if hal > 0:
                        S.op("dve", lambda e: e.tensor_copy(XN[:, :, PAD:PAD + hal], XH[:, :, 0:hal]), reads=["XH"], writes=xnr(PAD, hal))
                    prenorm(l, ni, o0, w1 - o0, PAD + hal)
                else:
                    prenorm(l, ni, w0, w1 - w0, PAD)
                rc = PAD + (w1 - w0)
                S.op("dve", lambda e: e.memset(XN[:, :, rc:rc + PAD], 0.0), writes=xnr(rc, PAD))
                if gi == 0:
                    S.op("dve", lambda e: e.memset(XN[:, :, 0:PAD], 0.0), writes=xnr(0, PAD))
                if gi < ng - 1 and hal > 0:
                    cs = PAD + (o1 - hal - w0)
                    S.op("dve", lambda e: e.tensor_copy(XH[:, :, 0:hal], XN[:, :, cs:cs + hal]), reads=xnr(cs, hal), writes=["XH"])
                if gi == 0 and after_first_pre is not None:
                    after_first_pre()
                body(o0, gsize, w0, w1 - w0, PAD + (o0 - w0))

        def ffn(l):
            GS = 640
            NG = T // GS
            ycols = split_cols(GS + 2, 512)
            wi = lambda k, ch: CVF[:, (l * 3 + k) * 44 + ch:(l * 3 + k) * 44 + ch + 1]

            def window(g):
                o0 = g * GS
                w0 = max(0, o0 - 1)
                w1 = min(T, o0 + GS + 1)
                base = (g % 2) * 720
                return o0, w0, w1, base + PAD + (o0 - w0), base

            def pre(g):
                o0, w0, w1, oc0, base = window(g)
                head, units = [], []
                if g > 0:
                    po0, pw0, pw1, poc0, pbase = window(g - 1)
                    cs = poc0 + GS - 1
                    head.append(lambda: S.op("dve", lambda e: e.tensor_copy(XN[:, :, base + PAD:base + PAD + 1], XN[:, :, cs:cs + 1]), reads=xnr(cs, 1), writes=xnr(base + PAD, 1)))
                    units = pre_tasks(l, 4, o0, w1 - o0, base + PAD + 1, sq_eng="pool")
                else:
                    head.append(lambda: S.op("dve", lambda e: e.memset(XN[:, :, base:base + PAD], 0.0), writes=xnr(base, PAD)))
                    units = pre_tasks(l, 4, w0, w1 - w0, base + PAD)
                rc = base + PAD + (w1 - w0)
                head.append(lambda: S.op("dve", lambda e: e.memset(XN[:, :, rc:rc + PAD], 0.0), writes=xnr(rc, PAD)))
                return head, units

            def stageA(g, i, W, wres):
                o0, w0, w1, oc0, base = window(g)
                bb = i % 2
                ci = i % 2
                for which, Yb in ((0, YGb[bb]), (1, YUb[bb])):
                    for (y0, yn) in ycols:
                        pt, pres, _ = ps_full()
                        cc = (which * 2 + ci) * 128
                        mm(pt[:, 0:yn], pres, [(W[:, k, cc:cc + 128], XN[:, k, oc0 - 1 + y0:oc0 - 1 + y0 + yn]) for k in range(8)],
                           [wres] + xnr(oc0 - 1 + y0, yn))
                        S.op("act", lambda e, Yb=Yb, pt=pt, y0=y0, yn=yn: e.activation(Yb[:, 1 + y0:1 + y0 + yn], pt[:, 0:yn], AF.Copy),
                             reads=pres, writes=[("Y", which, bb)])

            def stageB(g, i):
                bb = i % 2
                YG, YU, CG = YGb[bb], YUb[bb], CGb[bb][:, 0:GS]
                rY0, rY1, rCG = ("Y", 0, bb), ("Y", 1, bb), ("CG", bb)
                S.op("dve", lambda e: e.tensor_scalar(CG, YG[:, 2:GS + 2], wi(1, i), None, op0=ALU.mult), reads=[rY0, "CVF"], writes=[rCG])
                S.op("dve", lambda e: e.scalar_tensor_tensor(CG, YG[:, 1:GS + 1], wi(0, i), CG, op0=ALU.mult, op1=ALU.add), reads=[rY0, rCG, "CVF"], writes=[rCG])
                S.op("dve", lambda e: e.scalar_tensor_tensor(CG, YG[:, 3:GS + 3], wi(2, i), CG, op0=ALU.mult, op1=ALU.add), reads=[rY0, rCG, "CVF"], writes=[rCG])
                S.op("act", lambda e: e.activation(CG, CG, AF.Silu), reads=[rCG], writes=[rCG])
                CU = YG[:, 2:GS + 2]
                CU = YG[:, 0:GS]
                S.op("dve", lambda e: e.tensor_scalar(CU, YU[:, 2:GS + 2], wi(1, 22 + i), None, op0=ALU.mult), reads=[rY1, rY0, "CVF"], writes=[rY0])
                S.op("dve", lambda e: e.scalar_tensor_tensor(CU, YU[:, 1:GS + 1], wi(0, 22 + i), CU, op0=ALU.mult, op1=ALU.add), reads=[rY1, rY0, "CVF"], writes=[rY0])
                S.op("dve", lambda e: e.scalar_tensor_tensor(CU, YU[:, 3:GS + 3], wi(2, 22 + i), CU, op0=ALU.mult, op1=ALU.add), reads=[rY1, rY0, "CVF"], writes=[rY0])
                S.op("dve", lambda e: e.tensor_tensor(HIDv[:, i, 0:GS], CG, CU, op=ALU.mult), reads=[rCG, rY0], writes=[("HID", i)])

            def post_units(g):
                o0 = g * GS
                tasks = []
                for (u0, un) in split_cols(GS, 256):
                    tasks += post_tasks(l, 5, o0 + u0, un, [TSv[:, c, u0:u0 + un] for c in range(8)], [[("TS", c)] for c in range(8)], sq_eng="pool")
                return tasks

            def pipelined(units):
                q = [units[k:k + 4] for k in range(0, len(units), 4)]
                if not q:
                    return []
                out = [q[0][0], q[0][1]]
                for u in range(1, len(q)):
                    out += [q[u][0], q[u - 1][2], q[u - 1][3], q[u][1]]
                out += [q[-1][2], q[-1][3]]
                return out

            h0, u0_ = pre(0)
            for t in h0 + u0_:
                t()
            S.fence([g_misc])
            pending = []
            post_u = []
            for g in range(NG):
                nxt_h, nxt_u = pre(g + 1) if g + 1 < NG else ([], [])
                pending = nxt_h + pipelined(post_u + nxt_u)
                W = wres = None
                for i in range(22):
                    if i % 2 == 0:
                        W, wres = wslot(f"f{l}_up{i // 2}")
                    stageA(g, i, W, wres)
                    if i > 0:
                        stageB(g, i - 1)
                    if pending and i >= 1:
                        for _ in range(-(-len(pending) // (22 - i))):
                            pending.pop(0)()
                stageB(g, 21)
                while pending:
                    pending.pop(0)()
                for oc in range(8):
                    W, wres = wslot(f"f{l}_dn{oc}")
                    for (c0, cn) in split_cols(GS, 512):
                        pt, pres, _ = ps_full()
                        mm(pt[:, 0:cn], pres, [(W[:, k, :], HIDv[:, k, c0:c0 + cn]) for k in range(22)], [wres], pair_reads=[[("HID", k)] for k in range(22)])
                        S.op("act", lambda e, pt=pt, c0=c0, cn=cn, oc=oc: e.activation(TSv[:, oc, c0:c0 + cn], pt[:, 0:cn], AF.Copy), reads=pres, writes=[("TS", oc)])
                post_u = post_units(g)
            for t in post_u:
                t()

        def xattn_setup(l, ti):
            S.dma(g_misc, STG[:, 0:2048], memt[ti].rearrange("(k p) m -> p k m", p=128), writes=["STG"])
            S.op("act", lambda e: e.activation(MEMB.rearrange("p k m -> p (k m)"), STG[:, 0:2048], AF.Copy), reads=["STG"], writes=["MEMB"])

        def xattn_kv(l):
            W, wres = wslot(f"x{l}_k")
            for h in range(4):
                pt, pres = ps_half()
                mm(pt, pres, [(W[:, k, h * 128:(h + 1) * 128], MEMB[:, k, :]) for k in range(8)], [wres, "MEMB"])
                S.op("act", lambda e, h=h, pt=pt: e.activation(KX[:, h, :], pt, AF.Copy), reads=pres, writes=["KX"])
            W, wres = wslot(f"x{l}_v")
            for mb in range(2):
                pt, pres, _ = ps_full()
                mm(pt, pres, [(MEMB[:, k, mb * 128:(mb + 1) * 128], W[:, k, :]) for k in range(8)], [wres, "MEMB"])
                S.op("act", lambda e, mb=mb, pt=pt: e.activation(VX[:, mb, :], pt, AF.Copy), reads=pres, writes=["VX"])


        def xattn(l):
            def body(o0, gs, w0, wl, oc0):
                subs = split_cols(gs, 512)
                Wq, wqr = wslot(f"x{l}_q")
                Wo, wor = wslot(f"x{l}_o", pin=(f"x{l}_q",))

                def stQ(si):
                    s0, sn = subs[si]
                    QX = QXb[si % 2]
                    for h in range(4):
                        pt, pres, _ = ps_full()
                        mm(pt[:, 0:sn], pres, [(Wq[:, k, h * 128:(h + 1) * 128], XN[:, k, oc0 + s0:oc0 + s0 + sn]) for k in range(8)], [wqr] + xnr(oc0 + s0, sn))
                        S.op("act", lambda e, h=h, pt=pt: e.activation(QX[:, h, 0:sn], pt[:, 0:sn], AF.Copy), reads=pres, writes=[("QX", si % 2, h)])

                def stA(si, h):
                    s0, sn = subs[si]
                    QX = QXb[si % 2]
                    pb = (si * 4 + h) % 2
                    PX = PXb[pb]
                    for mb in range(2):
                        pt, pres, _ = ps_full()
                        mm(pt[:, 0:sn], pres, [(KX[:, h, mb * 128:(mb + 1) * 128], QX[:, h, 0:sn])], ["KX", ("QX", si % 2, h)])
                        S.op("act", lambda e, mb=mb, pt=pt: e.activation(PX[:, mb, 0:sn], pt[:, 0:sn], AF.Exp, scale=float(128 ** -0.5)),
                             reads=pres, writes=[("PX", pb, mb)])

                def stB(si, h):
                    s0, sn = subs[si]
                    pb = (si * 4 + h) % 2
                    PX, RD, OX = PXb[pb], RDb[pb], OXb[si % 2]
                    pxr = [("PX", pb, 0), ("PX", pb, 1)]
                    po, pores, _ = ps_full()
                    mm(po[:, 0:sn], pores, [(VX[:, mb, h * 128:(h + 1) * 128], PX[:, mb, 0:sn]) for mb in range(2)], ["VX"] + pxr)
                    pd, pdres, _ = ps_full()
                    mm(pd[:, 0:sn], pdres, [(ONES[:], PX[:, mb, 0:sn]) for mb in range(2)], ["ONES"] + pxr)
                    S.op("act", lambda e: e.activation(RD[:, 0:sn], pd[:, 0:sn], AF.Ln), reads=pdres, writes=[("RD", pb)])
                    S.op("act", lambda e: e.activation(RD[:, 0:sn], RD[:, 0:sn], AF.Exp, scale=-1.0), reads=[("RD", pb)], writes=[("RD", pb)])
                    S.op("dve", lambda e: e.tensor_tensor(OX[:, h, 0:sn], po[:, 0:sn], RD[:, 0:sn], op=ALU.mult), reads=pores + [("RD", pb)], writes=[("OX", si % 2, h)])

                def stO(si):
                    s0, sn = subs[si]
                    OX = OXb[si % 2]
                    for (u0, un) in split_cols(sn, 256):
                        tl, tr = [], []
                        for oc in range(8):
                            pt, pres = ps_half()
                            mm(pt[:, 0:un], pres, [(Wo[:, h, oc * 128:(oc + 1) * 128], OX[:, h, u0:u0 + un]) for h in range(4)], [wor],
                               pair_reads=[[("OX", si % 2, h)] for h in range(4)])
                            tl.append(pt[:, 0:un]); tr.append(pres)
                        postnorm_unit(l, 3, o0 + s0 + u0, un, tl, tr)

                n = len(subs)
                stQ(0)
                for si in range(n):
                    stA(si, 0)
                    if si > 0:
                        stO(si - 1)
                    stA(si, 1); stB(si, 0)
                    stA(si, 2); stB(si, 1)
                    stA(si, 3); stB(si, 2)
                    if si + 1 < n:
                        stQ(si + 1)
                    stB(si, 3)
                stO(n - 1)

            def first():
                S.fence([g_misc], engines=("act", "dve", "pool", "sp"))
                xattn_setup(l, ti_cur[0])
                xattn_kv(l)
            run_groups(l, 2, 1280, 0, body, after_first_pre=first)


        def conformer(l):
            j = l // 2
            def body(o0, gs, w0, wl, oc0):
                S.op("dve", lambda e: e.memset(Ubuf, 0.0), reads=[("U", c) for c in range(8)], writes=[("U", c) for c in range(8)])
                for pj in range(4):
                    W, wres = wslot(f"cf{j}_pw1_{pj}")
                    for ci in range(2):
                        c = pj * 2 + ci
                        for (s0, sn) in split_cols(wl, 512):
                            pa, pares, _ = ps_full()
                            mm(pa[:, 0:sn], pares, [(W[:, k, ci * 128:(ci + 1) * 128], XN[:, k, PAD + s0:PAD + s0 + sn]) for k in range(8)], [wres] + xnr(PAD + s0, sn))
                            pg, pgres, _ = ps_full()
                            mm(pg[:, 0:sn], pgres, [(W[:, k, 256 + ci * 128:256 + (ci + 1) * 128], XN[:, k, PAD + s0:PAD + s0 + sn]) for k in range(8)], [wres] + xnr(PAD + s0, sn))
                            S.op("act", lambda e, pg=pg: e.activation(SG[:, 0:sn], pg[:, 0:sn], AF.Sigmoid), reads=pgres, writes=["SG"])
                            S.op("dve", lambda e, pa=pa, c=c, s0=s0, sn=sn: e.tensor_tensor(Ubuf[:, c, PAD + s0:PAD + s0 + sn], pa[:, 0:sn], SG[:, 0:sn], op=ALU.mult),
                                 reads=pares + ["SG"], writes=[("U", c)])
                for c in range(8):
                    W, wres = wslot(f"cf{j}_dg{c}")
                    for (s0, sn) in split_cols(gs, 512):
                        pt, pres, _ = ps_full()
                        mm(pt[:, 0:sn], pres, [(W[:, k, :], Ubuf[:, c, oc0 + s0 - 15 + k:oc0 + s0 - 15 + k + sn]) for k in range(31)], [wres, ("U", c)])
                        S.op("act", lambda e, pt=pt, c=c, s0=s0, sn=sn: e.activation(CO[:, c, s0:s0 + sn], pt[:, 0:sn], AF.Copy), reads=pres, writes=[("CO", c)])
                cor = [("CO", c) for c in range(8)]
                W0, w0r = wslot(f"cf{j}_pw2_0")
                W1, w1r = wslot(f"cf{j}_pw2_1", pin=(f"cf{j}_pw2_0",))

                def LN(u0, un):
                    S.op("act", lambda e: e.activation(SQL[:, :, 0:un], CO[:, :, u0:u0 + un], AF.Square), reads=cor, writes=["SQL"])
                    pm, pmres = ps_half()
                    mm(pm[:, 0:un], pmres, [(ONESD[:], CO[:, c, u0:u0 + un]) for c in range(8)], cor + ["ONESD"])
                    pe2, pe2res = ps_half()
                    mm(pe2[:, 0:un], pe2res, [(ONESD[:], SQL[:, c, 0:un]) for c in range(8)], ["SQL", "ONESD"])
                    S.op("act", lambda e: e.activation(LT[:, 0, 0:un], pm[:, 0:un], AF.Copy), reads=pmres, writes=["LT0"])
                    S.op("dve", lambda e: e.tensor_tensor(LT[:, 1, 0:un], LT[:, 0, 0:un], LT[:, 0, 0:un], op=ALU.mult), reads=["LT0"], writes=["LT1"])
                    S.op("dve", lambda e: e.tensor_tensor(LT[:, 1, 0:un], pe2[:, 0:un], LT[:, 1, 0:un], op=ALU.subtract), reads=pe2res + ["LT1"], writes=["LT1"])
                    S.op("act", lambda e: e.activation(LT[:, 2, 0:un], LT[:, 1, 0:un], AF.Ln, bias=EPSC[:, 0:1]), reads=["LT1", "EPSC"], writes=["LT2"])
                    S.op("act", lambda e: e.activation(LT[:, 3, 0:un], LT[:, 2, 0:un], AF.Exp, scale=-0.5), reads=["LT2"], writes=["LT3"])
                    for c in range(8):
                        db = c % 2
                        S.op("dve", lambda e, c=c, db=db: e.tensor_tensor(DT[:, db, 0:un], CO[:, c, u0:u0 + un], LT[:, 0, 0:un], op=ALU.subtract), reads=[("CO", c), "LT0"], writes=[("DT", db)])
                        S.op("dve", lambda e, db=db: e.tensor_tensor(DT[:, db, 0:un], DT[:, db, 0:un], LT[:, 3, 0:un], op=ALU.mult), reads=[("DT", db), "LT3"], writes=[("DT", db)])
                        S.op("act", lambda e, c=c, db=db: e.activation(XN[:, c, oc0 + u0:oc0 + u0 + un], DT[:, db, 0:un], AF.Silu,
                                                                       bias=LNB[:, j * 8 + c:j * 8 + c + 1], scale=LNG[:, j * 8 + c:j * 8 + c + 1]),
                             reads=[("DT", db), "LNG", "LNB"], writes=xnr(oc0 + u0, un))

                def PW(u0, un):
                    tl, tr = [], []
                    tiles, own = t_tiles()
                    for oc in range(8):
                        Wp, wr_ = (W0, w0r) if oc < 4 else (W1, w1r)
                        pt, pres = tiles[oc]
                        mm(pt[:, 0:un], pres, [(Wp[:, k, (oc % 4) * 128:(oc % 4 + 1) * 128], XN[:, k, oc0 + u0:oc0 + u0 + un]) for k in range(8)], [wr_] + xnr(oc0 + u0, un))
                        tl.append(pt[:, 0:un]); tr.append(pres)
                    tk = post_tasks(l, 1, o0 + u0, un, tl, tr)
                    tk[0]()
                    return tk[1:] + [lambda: pinned.difference_update(own)]

                units = split_cols(gs, 256)
                rest = []
                for ui, (u0, un) in enumerate(units):
                    LN(u0, un)
                    for t in rest:
                        t()
                    rest = []
                    if ui > 0:
                        rest = PW(*units[ui - 1])
                for t in rest:
                    t()
                for t in PW(*units[-1]):
                    t()
            run_groups(l, 0, 1280, 16, body, after_first_pre=lambda: S.fence([g_misc]))

        NBW = 11

        def abmix(l):
            j = l // 2
            def body(o0, gs, w0, wl, oc0):
                if "0" in DBG:
                    return
                nbw = wl // 128
                ob0 = (o0 - w0) // 128
                S.op("dve", lambda e: e.memset(PC, 0.0), reads=[("PC", c) for c in range(4)], writes=[("PC", c) for c in range(4)])
                W, wres = wslot(f"ab{j}_kv")
                if "w" in DBG:
                    return
                for g in range(2):
                    for (s0, sn) in split_cols(wl, 512):
                        pt, pres, _ = ps_full()
                        mm(pt[:, 0:sn], pres, [(W[:, k, g * 128:(g + 1) * 128], XN[:, k, PAD + s0:PAD + s0 + sn]) for k in range(8)], [wres] + xnr(PAD + s0, sn))
                        S.op("act", lambda e, pt=pt, g=g, s0=s0, sn=sn: e.activation(KT[:, g, s0:s0 + sn], pt[:, 0:sn], AF.Copy), reads=pres, writes=[("KT", g)])
                if "k" in DBG:
                    return
                for b in range(nbw if "v" not in DBG else 1):
                    pt, pres = ps_half()
                    mm(pt[:, 0:128], pres, [(XN[:, k, PAD + b * 128:PAD + (b + 1) * 128], W[:, k, 256:384]) for k in range(8)], [wres] + xnr(PAD + b * 128, 128))
                    S.op("act", lambda e, pt=pt, b=b: e.activation(VV[:, b, :], pt[:, 0:128], AF.Copy), reads=pres, writes=["VV"])
                if "1" in DBG:
                    return
                W, wres = wslot(f"ab{j}_gc")
                for c in range(4):
                    for (s0, sn) in split_cols(wl, 512):
                        pt, pres, _ = ps_full()
                        mm(pt[:, 0:sn], pres, [(W[:, k, c * 128:(c + 1) * 128], XN[:, k, PAD + s0:PAD + s0 + sn]) for k in range(8)], [wres] + xnr(PAD + s0, sn))
                        S.op("act", lambda e, pt=pt, c=c, s0=s0, sn=sn: e.activation(PC[:, c, PAD + s0:PAD + s0 + sn], pt[:, 0:sn], AF.Copy), reads=pres, writes=[("PC", c)])
                W, wres = wslot(f"ab{j}_xa")
                for c in range(4):
                    for (s0, sn) in split_cols(wl, 512):
                        pt, pres, _ = ps_full()
                        mm(pt[:, 0:sn], pres, [(W[:, k, c * 128:(c + 1) * 128], XN[:, k, PAD + s0:PAD + s0 + sn]) for k in range(8)], [wres] + xnr(PAD + s0, sn))
                        S.op("dve", lambda e, pt=pt, c=c, s0=s0, sn=sn: e.tensor_tensor(PC[:, c, PAD + s0:PAD + s0 + sn], pt[:, 0:sn], PC[:, c, PAD + s0:PAD + s0 + sn], op=ALU.mult),
                             reads=pres + [("PC", c)], writes=[("PC", c)])
                for c in range(4):
                    cw = lambda k, c=c: CVA[:, (j * 3 + k) * 4 + c:(j * 3 + k) * 4 + c + 1]
                    for (s0, sn) in split_cols(gs, 512):
                        pc0 = oc0 + s0
                        S.op("dve", lambda e: e.tensor_scalar(GB[:, c, s0:s0 + sn], PC[:, c, pc0:pc0 + sn], cw(1), None, op0=ALU.mult), reads=[("PC", c), "CVA"], writes=[("gb", c)])
                        S.op("dve", lambda e: e.scalar_tensor_tensor(GB[:, c, s0:s0 + sn], PC[:, c, pc0 - 1:pc0 - 1 + sn], cw(0), GB[:, c, s0:s0 + sn], op0=ALU.mult, op1=ALU.add), reads=[("PC", c), "CVA", ("gb", c)], writes=[("gb", c)])
                        S.op("dve", lambda e: e.scalar_tensor_tensor(GB[:, c, s0:s0 + sn], PC[:, c, pc0 + 1:pc0 + 1 + sn], cw(2), GB[:, c, s0:s0 + sn], op0=ALU.mult, op1=ALU.add), reads=[("PC", c), "CVA", ("gb", c)], writes=[("gb", c)])
                W, wres = wslot(f"ab{j}_q")
                for c in range(4):
                    for (s0, sn) in split_cols(gs, 512):
                        pt, pres, _ = ps_full()
                        mm(pt[:, 0:sn], pres, [(W[:, k, c * 128:(c + 1) * 128], XN[:, k, oc0 + s0:oc0 + s0 + sn]) for k in range(8)], [wres] + xnr(oc0 + s0, sn))
                        S.op("act", lambda e, pt=pt, c=c, s0=s0, sn=sn: e.activation(QB[:, c, s0:s0 + sn], pt[:, 0:sn], AF.Copy), reads=pres, writes=[("q", c)])
                W, wres = wslot(f"ab{j}_gb")
                for c in range(4):
                    for (s0, sn) in split_cols(gs, 512):
                        pt, pres, _ = ps_full()
                        mm(pt[:, 0:sn], pres, [(W[:, k, c * 128:(c + 1) * 128], XN[:, k, oc0 + s0:oc0 + s0 + sn]) for k in range(8)], [wres] + xnr(oc0 + s0, sn))
                        S.op("dve", lambda e, pt=pt, c=c, s0=s0, sn=sn: e.tensor_tensor(GB[:, c, s0:s0 + sn], pt[:, 0:sn], GB[:, c, s0:s0 + sn], op=ALU.mult),
                             reads=pres + [("gb", c)], writes=[("gb", c)])
                if "2" in DBG:
                    return
                WA, war = wslot(f"ab{j}_woA")
                WB0, wb0r = wslot(f"ab{j}_woB0", pin=(f"ab{j}_woA",))
                WB1, wb1r = wslot(f"ab{j}_woB1", pin=(f"ab{j}_woA", f"ab{j}_woB0"))
                nq = gs // 128

                def stageS(qn):
                    qb = ob0 + qn
                    qc = qn * 128
                    pb = qn % 2
                    js = [jj for jj in (qb - 1, qb, qb + 1) if 0 <= jj < nbw]
                    for jj in js:
                        o = qb - jj + 1
                        js_ = jj - qb + 1
                        banks = [ps_full(), ps_full()]
                        for hp in range(2):
                            pt, pres, _ = banks[hp]
                            S.op("pe", lambda e, pt=pt, hp=hp, o=o: e.matmul(pt, lhsT=IDENT[:], rhs=EBB[:, hp, o, :, :].rearrange("p c q -> p (c q)"), start=True, stop=False),
                                 reads=["IDENT", "EB"], writes=pres, inc=False)
                        for h in range(8):
                            c, hp, gh = h // 2, h % 2, h // 4
                            pt, pres, _ = banks[hp]
                            S.op("pe", lambda e, pt=pt, c=c, hp=hp, gh=gh, jj=jj: e.matmul(
                                pt[:, c * 128:(c + 1) * 128], lhsT=KT[hp * 64:(hp + 1) * 64, gh, jj * 128:(jj + 1) * 128],
                                rhs=QB[hp * 64:(hp + 1) * 64, c, qc:qc + 128], start=False, stop=True),
                                reads=[("KT", 0), ("KT", 1)] + [("q", cc) for cc in range(4)] if h < 2 else (), writes=pres, inc=(h == 7))
                        for hp in range(2):
                            pt, pres, _ = banks[hp]
                            S.op("act", lambda e, pt=pt, hp=hp, js_=js_: e.activation(PT5[pb][:, js_, :, hp, :], pt.rearrange("p (c q) -> p c q", c=4), AF.Exp, scale=0.125),
                                 reads=pres, writes=[("PT", pb, js_)])

                def stageV(qn):
                    qb = ob0 + qn
                    pb = qn % 2
                    qi = qn % 2
                    js = [jj for jj in (qb - 1, qb, qb + 1) if 0 <= jj < nbw]
                    ptr = [("PT", pb, jj - qb + 1) for jj in js]
                    for gh in range(2):
                        po, pores, _ = ps_full()
                        mm(po[0:64, :], pores, [(VV[:, jj, gh * 64:(gh + 1) * 64], PTh[pb][:, jj - qb + 1, gh * 4:(gh + 1) * 4, :].rearrange("p h q -> p (h q)")) for jj in js],
                           ["VV"] + ptr)
                        pd, pdres, _ = ps_full()
                        prs = [(ONES[:, 0:64], PTh[pb][:, jj - qb + 1, gh * 4:(gh + 1) * 4, :].rearrange("p h q -> p (h q)")) for jj in js]
                        prs.append((ONES[0:1, 0:64], ESROW[0:1, (j * 2 + gh) * 512:(j * 2 + gh + 1) * 512]))
                        mm(pd[0:64, :], pdres, prs, ["ONES", "ESROW"] + ptr)
                        S.op("act", lambda e, pd=pd: e.activation(RDEN[0:64, :], pd[0:64, :], AF.Ln), reads=pdres, writes=[("PC", 0)])
                        S.op("act", lambda e: e.activation(RDEN[0:64, :], RDEN[0:64, :], AF.Exp, scale=-1.0), reads=[("PC", 0)], writes=[("PC", 0)])
                        S.op("dve", lambda e, po=po, gh=gh, qi=qi: e.tensor_tensor(
                            YB[0:64, gh * 4:(gh + 1) * 4, qi * 128:(qi + 1) * 128], po[0:64, :].rearrange("p (h q) -> p h q", h=4),
                            RDEN[0:64, :].rearrange("p (h q) -> p h q", h=4), op=ALU.mult), reads=pores + [("PC", 0)], writes=[("YB", qi)])

                def stageO(un_i):
                    s0 = un_i * 256
                    tl, tr = [], []
                    tiles, own = t_tiles()
                    for oc in range(8):
                        pt, pres = tiles[oc]
                        prs = [(WA[:, c, oc * 128:(oc + 1) * 128], GB[:, c, s0:s0 + 256]) for c in range(4)]
                        WBx = WB0 if oc < 4 else WB1
                        prs += [(WBx[:, h, (oc % 4) * 128:(oc % 4 + 1) * 128], YB[0:64, h, 0:256]) for h in range(8)]
                        mm(pt[:, 0:256], pres, prs, [war, wb0r, wb1r], pair_reads=[[("gb", c)] for c in range(4)] + [[("YB", 0), ("YB", 1)] for h in range(8)])
                        tl.append(pt[:, 0:256]); tr.append(pres)
                    tk = post_tasks(l, 1, o0 + s0, 256, tl, tr)
                    tk[0]()
                    return tk[1:] + [lambda: pinned.difference_update(own)]

                if "3" in DBG:
                    nq = 0
                rest = []
                for qn in range(nq):
                    stageS(qn)
                    for t in rest:
                        t()
                    rest = []
                    if qn > 0:
                        stageV(qn - 1)
                        if (qn - 1) % 2 == 1:
                            rest = stageO((qn - 1) // 2)
                if nq:
                    for t in rest:
                        t()
                    stageV(nq - 1)
                    for t in stageO((nq - 1) // 2):
                        t()
            run_groups(l, 0, 1280, 128, body, after_first_pre=lambda: S.fence([g_misc, g_cv] if first_ab[0] else [g_misc]))
            first_ab[0] = False

        ti_cur = [0]
        first_ab = [True]
        for ti in range(n_tiles):
            allH = [("H", b) for b in range(T // 128)]
            for q4 in range(4):
                c0_, c1_ = q4 * 640, (q4 + 1) * 640
                S.dma(g_ios[q4], H[:, :, c0_:c1_], xt[ti, :, c0_:c1_].rearrange("(c p) t -> p c t", p=128), writes=Hres(c0_, 640))
            sub = 0
            for l in range(depth):
                if sub >= n_sub:
                    break
                ti_cur[0] = ti
                if "c" in DBG:
                    pass
                elif l % 2 == 0:
                    abmix(l)
                else:
                    conformer(l)
                sub += 1
                if sub >= n_sub:
                    break
                if True:
                    xattn(l)
                sub += 1
                if sub >= n_sub:
                    break
                ffn(l)
                sub += 1
            for q4 in range(4):
                c0_, c1_ = q4 * 640, (q4 + 1) * 640
                S.dma(g_ios[q4], yt[ti, :, c0_:c1_].rearrange("(c p) t -> p c t", p=128), H[:, :, c0_:c1_], reads=Hres(c0_, 640), writes=[("yt", ti, q4)])
        S.wait_all("sp", [("yt", ti, q4) for ti in range(n_tiles) for q4 in range(4)])
    return nc


def tile_plan():
    tiles = []
    def plan(S_, n):
        return [int(round(i * (S_ - T) / (n - 1))) for i in range(n)]
    for s in plan(16384, 8):
        tiles.append(("p", 0, s, 16384))
    for b in range(4):
        for s in plan(8192, 4):
            tiles.append(("s", b, s, 8192))
    return tiles


def _t5_bucket(rel):
    half, max_exact = 16, 8
    ret = (rel > 0).astype(np.int32) * half
    n = np.abs(rel)
    nf = np.maximum(n, 1).astype(np.float32)
    large = max_exact + (np.log(nf / max_exact) / np.float32(np.log(128 / max_exact)) * (half - max_exact)).astype(np.int32)
    large = np.minimum(large, half - 1)
    return ret + np.where(n < max_exact, n, large)


def _fm(a, nchunk):
    lead = a.shape[:-1]
    r = a.reshape(*lead, nchunk, 128)
    r = np.moveaxis(r, -1, 0)
    return np.ascontiguousarray(r.reshape(128, -1))


def host_consts(inp):
    c = {}
    c["gains"] = _fm(np.asarray(inp["norm_g"], np.float32), 8)
    c["cva"] = _fm(np.asarray(inp["conv_a"], np.float32), 4)
    c["cvc"] = _fm(np.asarray(inp["conv_c"], np.float32), 8)
    c["lng"] = _fm(np.asarray(inp["ln_g_c"], np.float32), 8)
    c["lnb"] = _fm(np.asarray(inp["ln_b_c"], np.float32), 8)
    c["cvf"] = _fm(np.asarray(inp["conv_f"], np.float32), 44)
    sk = np.asarray(inp["sink_b"], np.float32)
    c["sinkrow"] = np.ascontiguousarray(np.repeat(sk.reshape(2, 2, 4, 1), 128, axis=3).reshape(1, 2048))
    jj = np.arange(128)[:, None]
    x = np.arange(384)[None, :]
    rel = jj - (x % 128) - (x // 128 - 1) * 128
    bidx = _t5_bucket(rel)
    rb = np.asarray(inp["rel_bias"], np.float32)
    bt = rb[bidx]
    c["biasT"] = np.ascontiguousarray(np.transpose(bt, (0, 2, 1)).reshape(128, 8 * 384))
    c["bmask"] = (np.abs(rel) <= 128).astype(np.float32)
    c["ident"] = np.eye(128, dtype=np.float32)
    return c


_NC_CACHE = {}


def kernel(**inp):
    tiles = tile_plan()
    xp = np.asarray(inp["x_prompt"], np.float32)
    xs = np.asarray(inp["x_sample"], np.float32)
    mp = np.asarray(inp["mem_prompt"], np.float32)
    ms = np.asarray(inp["mem_sample"], np.float32)
    consts = host_consts(inp)
    wnames = ["w_in_ab", "w_out_ab", "w_pw1_c", "w_pw2_c", "w_xq", "w_xkv", "w_xo", "w_up", "w_down"]
    weights = {k: np.ascontiguousarray(np.asarray(inp[k], np.float32)) for k in wnames}
    in_maps = []
    for core in range(NCORES):
        xt = np.empty((NTC, D, T), np.float32)
        mt = np.empty((NTC, D, 256), np.float32)
        for i in range(NTC):
            grp, b, s, _ = tiles[core * NTC + i]
            src = xp if grp == "p" else xs
            mem = mp if grp == "p" else ms
            xt[i] = src[b, s:s + T, :].T
            mt[i] = mem[b].T
        m = {"xt": xt, "memt": mt}
        m.update(weights)
        m.update(consts)
        in_maps.append(m)
    if "nc" not in _NC_CACHE:
        rec = []
        build_program(n_sub=12, record=rec)
        tg = {}
        build_program(n_sub=12, wseq=rec, inc_record=tg)
        _NC_CACHE["nc"] = build_program(n_sub=12, wseq=rec, inc_targets=tg)
    res = run_bass_kernel_spmd(_NC_CACHE["nc"], in_maps, core_ids=list(range(NCORES)))
    yp = np.empty_like(xp)
    ys = np.empty_like(xs)
    for core in range(NCORES):
        yt = res.results[core]["yt"]
        for i in range(NTC):
            grp, b, s, L = tiles[core * NTC + i]
            lo = 0 if s == 0 else HALO
            hi = T if s + T == L else T - HALO
            dst = yp if grp == "p" else ys
            dst[b, s + lo:s + hi, :] = yt[i][:, lo:hi].T
    return (yp, ys)
```
